# Optimizing a Trainium2 kernel written in Bass

```python
import jax, jax.numpy as jnp
from jax import lax
import numpy as np

D_MODEL = 2048
BATCH = 8
SEQ = 4096
DEPTH = 1
DEC_BATCH = 16
DEC_SEQ = 64
PAST_LEN = 1024

CHUNK = 64
SGU_CHUNK = 128
SGU_WIDTH = 1024
SGU_GROUPS = 4
SGU_GDIM = SGU_WIDTH // SGU_GROUPS
POOL_WIDTH = 1024
POOL_WINDOWS = (2, 4, 8, 16)
POOL_GROUPS = len(POOL_WINDOWS)
POOL_GDIM = POOL_WIDTH // POOL_GROUPS
POOL_STATE = max(POOL_WINDOWS) - 1
N_MEM = 256
MEM_HEADS = 4
MEM_HDIM = 256
MEM_WIDTH = MEM_HEADS * MEM_HDIM
N_BRANCH = 3
D_FF = 5632
EPS = 1e-6
OFF_U = 0
OFF_V = SGU_WIDTH
OFF_B = 2 * SGU_WIDTH
OFF_Q = OFF_B + POOL_WIDTH
OFF_G = OFF_Q + MEM_WIDTH
IN_COLS = OFF_G + N_BRANCH * D_MODEL

kernel_name = 'gated_sgu_pool_memxattn_streaming_step'


def rmsnorm(x, g):
    xf = x.astype(jnp.float32)
    y = xf * lax.rsqrt(jnp.mean(xf * xf, axis=-1, keepdims=True) + EPS)
    return (y * g.astype(jnp.float32)).astype(x.dtype)


def spatial_gating(u, v, w_s, b_s, g_v, chunk_len):
    bsz, length, _ = v.shape
    n = length // chunk_len
    vn = rmsnorm(v, g_v).reshape(bsz, n, chunk_len, SGU_GROUPS, SGU_GDIM)
    idx = jnp.arange(chunk_len)
    mask = (idx[None, :] // CHUNK) <= (idx[:, None] // CHUNK)
    ws = jnp.where(mask[None], w_s[:, :chunk_len, :chunk_len], jnp.zeros((), w_s.dtype))
    mixed = jnp.einsum('gij,bnjgd->bnigd', ws, vn)
    mixed = mixed + jnp.transpose(b_s[:, :chunk_len])[None, None, :, :, None]
    return u * mixed.reshape(bsz, length, SGU_WIDTH)


def multiscale_pool(xb, prefix, pos0, w_pool, pool_scale):
    bsz, length, _ = xb.shape
    xp = jnp.concatenate([prefix.astype(xb.dtype), xb], axis=1)
    cs = jnp.cumsum(xp.astype(jnp.float32), axis=1)
    cs = jnp.concatenate([jnp.zeros((bsz, 1, POOL_WIDTH), jnp.float32), cs], axis=1)
    pos = pos0 + jnp.arange(length)
    xf = xb.astype(jnp.float32)
    outs = []
    for g, w in enumerate(POOL_WINDOWS):
        lo, hi = g * POOL_GDIM, (g + 1) * POOL_GDIM
        s = cs[:, POOL_STATE + 1:, lo:hi] - cs[:, POOL_STATE + 1 - w:POOL_STATE + 1 - w + length, lo:hi]
        cnt = jnp.minimum(pos + 1, w).astype(jnp.float32)
        pooled = s / cnt[None, :, None] - xf[..., lo:hi]
        outs.append(jnp.einsum('blc,cd->bld', pooled, w_pool[g].astype(jnp.float32)))
    out = jnp.concatenate(outs, axis=-1) * pool_scale.astype(jnp.float32)
    return out.astype(xb.dtype), xp[:, -POOL_STATE:]


def mem_kv(mem, g_mem, w_mk, w_mv):
    bsz = mem.shape[0]
    mn = rmsnorm(mem, g_mem)
    k = (mn @ w_mk).reshape(bsz, N_MEM, MEM_HEADS, MEM_HDIM)
    v = (mn @ w_mv).reshape(bsz, N_MEM, MEM_HEADS, MEM_HDIM)
    return k, v


def mem_attend(q, k, v):
    bsz, length, _ = q.shape
    qh = q.reshape(bsz, length, MEM_HEADS, MEM_HDIM)
    s = jnp.einsum('blhd,bmhd->bhlm', qh, k).astype(jnp.float32) * (MEM_HDIM ** -0.5)
    p = jax.nn.softmax(s, axis=-1).astype(v.dtype)
    o = jnp.einsum('bhlm,bmhd->blhd', p, v)
    return o.reshape(bsz, length, MEM_WIDTH)


def block(x, pool_prefix, pos0, mem_k, mem_v, chunk_len,
          g_mix, w_in, b_gate, g_sgu_v, w_sgu, b_sgu, w_pool, pool_scale,
          w_pa, w_pb, w_pc, w_o, g_ffn, w_ff_gate, w_ff_up, w_ff_down):
    bsz, length, _ = x.shape
    h = rmsnorm(x, g_mix)
    z = h @ w_in
    uv = jax.nn.gelu(z[..., OFF_U:OFF_B])
    u, v = uv[..., :SGU_WIDTH], uv[..., SGU_WIDTH:]
    xb = z[..., OFF_B:OFF_Q]
    q = z[..., OFF_Q:OFF_G]
    gates = jax.nn.sigmoid(z[..., OFF_G:] + b_gate).reshape(bsz, length, N_BRANCH, D_MODEL)
    ya = spatial_gating(u, v, w_sgu, b_sgu, g_sgu_v, chunk_len) @ w_pa
    pooled, new_pool = multiscale_pool(xb, pool_prefix, pos0, w_pool, pool_scale)
    yb = pooled @ w_pb
    yc = mem_attend(q, mem_k, mem_v) @ w_pc
    merged = gates[:, :, 0] * ya + gates[:, :, 1] * yb + gates[:, :, 2] * yc
    x = x + merged @ w_o
    hf = rmsnorm(x, g_ffn)
    x = x + (jax.nn.silu(hf @ w_ff_gate) * (hf @ w_ff_up)) @ w_ff_down
    return x, new_pool, v


def setup_inputs(seed: int = 0) -> dict:
    key = jax.random.key(seed)
    ks = jax.random.split(key, 32)
    f32 = jnp.float32
    def nrm(k, shape, scale=1.0):
        return jax.random.normal(k, shape, f32) * scale
    def gain(k, shape):
        return 1.0 + 0.05 * jax.random.normal(k, shape, f32)
    return {
        'x_prompt': nrm(ks[0], (BATCH, SEQ, D_MODEL)),
        'x_sample': nrm(ks[1], (DEC_BATCH, DEC_SEQ, D_MODEL)),
        'mem_prompt': nrm(ks[2], (BATCH, N_MEM, D_MODEL)),
        'state_pool': nrm(ks[3], (DEPTH, DEC_BATCH, POOL_STATE, POOL_WIDTH)),
        'cache_mem_k': nrm(ks[4], (DEPTH, DEC_BATCH, N_MEM, MEM_HEADS, MEM_HDIM)),
        'cache_mem_v': nrm(ks[5], (DEPTH, DEC_BATCH, N_MEM, MEM_HEADS, MEM_HDIM)),
        'g_mix': gain(ks[6], (DEPTH, D_MODEL)),
        'w_in': nrm(ks[7], (DEPTH, D_MODEL, IN_COLS), D_MODEL ** -0.5),
        'b_gate': nrm(ks[8], (DEPTH, N_BRANCH * D_MODEL), 0.02),
        'g_sgu_v': gain(ks[9], (DEPTH, SGU_WIDTH)),
        'w_sgu': nrm(ks[10], (DEPTH, SGU_GROUPS, SGU_CHUNK, SGU_CHUNK), SGU_CHUNK ** -0.5),
        'b_sgu': gain(ks[11], (DEPTH, SGU_GROUPS, SGU_CHUNK)),
        'w_pool': nrm(ks[12], (DEPTH, POOL_GROUPS, POOL_GDIM, POOL_GDIM), POOL_GDIM ** -0.5),
        'pool_scale': gain(ks[13], (DEPTH, POOL_WIDTH)),
        'g_mem': gain(ks[14], (DEPTH, D_MODEL)),
        'w_mk': nrm(ks[15], (DEPTH, D_MODEL, MEM_WIDTH), D_MODEL ** -0.5),
        'w_mv': nrm(ks[16], (DEPTH, D_MODEL, MEM_WIDTH), D_MODEL ** -0.5),
        'w_pa': nrm(ks[17], (DEPTH, SGU_WIDTH, D_MODEL), SGU_WIDTH ** -0.5),
        'w_pb': nrm(ks[18], (DEPTH, POOL_WIDTH, D_MODEL), POOL_WIDTH ** -0.5),
        'w_pc': nrm(ks[19], (DEPTH, MEM_WIDTH, D_MODEL), MEM_WIDTH ** -0.5),
        'w_o': nrm(ks[20], (DEPTH, D_MODEL, D_MODEL), D_MODEL ** -0.5),
        'g_ffn': gain(ks[21], (DEPTH, D_MODEL)),
        'w_ff_gate': nrm(ks[22], (DEPTH, D_MODEL, D_FF), D_MODEL ** -0.5),
        'w_ff_up': nrm(ks[23], (DEPTH, D_MODEL, D_FF), D_MODEL ** -0.5),
        'w_ff_down': nrm(ks[24], (DEPTH, D_FF, D_MODEL), D_FF ** -0.5),
        'g_final': gain(ks[25], (D_MODEL,)),
    }


def reference(x_prompt, x_sample, mem_prompt, state_pool, cache_mem_k, cache_mem_v,
              g_mix, w_in, b_gate, g_sgu_v, w_sgu, b_sgu, w_pool, pool_scale,
              g_mem, w_mk, w_mv, w_pa, w_pb, w_pc, w_o,
              g_ffn, w_ff_gate, w_ff_up, w_ff_down, g_final):
    xp, xs = x_prompt, x_sample
    dec_seq = x_sample.shape[1]
    pool_p, pool_s, mk_p, mv_p, v_s = [], [], [], [], []
    for l in range(DEPTH):
        lw = (g_mix[l], w_in[l], b_gate[l], g_sgu_v[l], w_sgu[l], b_sgu[l], w_pool[l], pool_scale[l],
              w_pa[l], w_pb[l], w_pc[l], w_o[l], g_ffn[l], w_ff_gate[l], w_ff_up[l], w_ff_down[l])
        mk, mv = mem_kv(mem_prompt, g_mem[l], w_mk[l], w_mv[l])
        zero_prefix = jnp.zeros((xp.shape[0], POOL_STATE, POOL_WIDTH), xp.dtype)
        xp, np_pool, _ = block(xp, zero_prefix, 0, mk, mv, SGU_CHUNK, *lw)
        pool_p.append(np_pool)
        mk_p.append(mk)
        mv_p.append(mv)
        xs, ns_pool, vrows = block(xs, state_pool[l], PAST_LEN, cache_mem_k[l], cache_mem_v[l], dec_seq, *lw)
        pool_s.append(ns_pool)
        v_s.append(vrows)
    y_prompt = rmsnorm(xp, g_final)
    y_sample = rmsnorm(xs, g_final)
    return (y_prompt, y_sample, jnp.stack(pool_p), jnp.stack(pool_s),
            jnp.stack(mk_p), jnp.stack(mv_p), jnp.stack(v_s))
```

```python
import numpy as np
import concourse.bass as bass
import concourse.mybir as mybir
from concourse.bass_utils import run_bass_kernel_spmd

F32 = mybir.dt.float32
BF16 = mybir.dt.bfloat16
AF = mybir.ActivationFunctionType
ALU = mybir.AluOpType
AX = mybir.AxisListType

D = 2048
SEQ = 4096
NCORE = 8
SGUW = 1024
POOLW = 1024
MEMW = 1024
NMEM = 256
DFF = 5632
INC = 10240
OFF_U, OFF_V, OFF_B, OFF_Q, OFF_G = 0, 1024, 2048, 3072, 4096
EPS = 1e-6
WINS = (2, 4, 8, 16)
PST = 15
NFC = DFF // 128

ENGS = ("pe", "act", "dve", "pool", "sp")


class Op:
    __slots__ = ("eng", "fn", "deps", "idx", "is_dma", "key", "flag", "count",
                 "ndma", "name")

    def __init__(self, eng, fn, is_dma, key, name):
        self.eng = eng
        self.fn = fn
        self.is_dma = is_dma
        self.key = key
        self.deps = []
        self.flag = False
        self.count = None
        self.ndma = 1
        self.name = name


class Prog:
    def __init__(self, nc):
        self.nc = nc
        self.ops = {e: [] for e in ENGS}
        self.last_w = {}
        self.readers = {}
        self.nops = 0
        self.last_dma_on_key = {}
        self.dma_keys = []
        self.known = {e: {} for e in ENGS}

    def op(self, eng, fn, reads=(), writes=(), dma_key=None, name="", ndma=1):
        is_dma = dma_key is not None
        o = Op(eng, fn, is_dma, dma_key, name)
        o.ndma = ndma
        o.idx = self.nops
        self.nops += 1
        deps = {}
        extra = [r + "#x" for r in reads if r.startswith("PS:")]
        if extra:
            writes = list(writes) + extra
        for r in reads:
            w = self.last_w.get(r)
            if w is not None:
                deps[w.idx] = w
        for r in writes:
            w = self.last_w.get(r)
            if w is not None:
                deps[w.idx] = w
            for rd in self.readers.get(r, ()):
                deps[rd.idx] = rd
        if is_dma:
            if dma_key not in self.last_dma_on_key:
                self.dma_keys.append(dma_key)
            prev = self.last_dma_on_key.get(dma_key)
            if prev is not None:
                deps[prev.idx] = prev
            self.last_dma_on_key[dma_key] = o
        kn = self.known[eng]
        for d in sorted(deps.values(), key=lambda x: x.idx):
            src = ("dma", d.key) if d.is_dma else ("eng", d.eng)
            if kn.get(src, -1) >= d.idx:
                continue
            kn[src] = d.idx
            o.deps.append(d)
            d.flag = True
        for r in reads:
            self.readers.setdefault(r, []).append(o)
        for r in writes:
            self.last_w[r] = o
            self.readers[r] = []
        self.ops[eng].append(o)
        return o

    def finish_on(self, eng, ops):
        o = self.op(eng, None, name="final")
        for d in ops:
            o.deps.append(d)
            d.flag = True
        return o

    def emit(self):
        nc = self.nc
        sems = {}
        for e in ("pe", "act", "dve", "pool"):
            sems[("eng", e)] = nc.alloc_semaphore("s_" + e)
        for k in self.dma_keys:
            sems[("dma", k)] = nc.alloc_semaphore("d_" + str(k))
        for e in ENGS:
            c = 0
            for o in self.ops[e]:
                if o.is_dma:
                    continue
                if o.flag:
                    c += 1
                    o.count = c
        dcount = {k: 0 for k in self.dma_keys}
        allops = []
        for e in ENGS:
            allops.extend(self.ops[e])
        allops.sort(key=lambda x: x.idx)
        for o in allops:
            if o.is_dma:
                dcount[o.key] += 16 * max(1, o.ndma)
                o.count = dcount[o.key]
        prog = self

        def run(engname, eng):
            for o in prog.ops[engname]:
                for d in o.deps:
                    if d.is_dma:
                        eng.wait_ge(sems[("dma", d.key)], d.count)
                    else:
                        eng.wait_ge(sems[("eng", d.eng)], d.count)
                if o.fn is None:
                    continue
                insts = o.fn(eng)
                if insts is None:
                    insts = []
                elif not isinstance(insts, (list, tuple)):
                    insts = [insts]
                if o.is_dma:
                    assert len(insts) == o.ndma, (o.name, len(insts), o.ndma)
                    for i in insts:
                        i.then_inc(sems[("dma", o.key)], 16)
                elif o.flag:
                    insts[-1].then_inc(sems[("eng", engname)], 1)

        with nc.Block() as block:
            @block.tensor
            def _(eng):
                run("pe", eng)

            @block.scalar
            def _(eng):
                run("act", eng)

            @block.vector
            def _(eng):
                run("dve", eng)

            @block.gpsimd
            def _(eng):
                run("pool", eng)

            @block.sync
            def _(eng):
                run("sp", eng)


C_GMIX, C_GFFN, C_BG, C_PSC, C_GV, C_GMEM, NCOLV = 0, 16, 32, 80, 88, 96, 112
TS = 384
NS = TS // 128


def default_plan():
    plan = []
    for t in range(10):
        plan.append([("p", t * TS + s * 128) for s in range(NS)])
    plan.append([("p", 3840), ("p", 3968), ("s", 0)])
    return plan


def build_program(plan=None):
    if plan is None:
        plan = default_plan()
    nc = bass.Bass("TRN2", target_bir_lowering=False)
    P = Prog(nc)

    def din(name, shape, dt=F32):
        return nc.dram_tensor(name, list(shape), dt, kind="ExternalInput").ap()

    def dout(name, shape):
        return nc.dram_tensor(name, list(shape), F32, kind="ExternalOutput").ap()

    def dscr(name, shape):
        return nc.dram_tensor(name, list(shape), BF16, kind="Internal").ap()

    xp = din("xp", [SEQ, D])
    xs = din("xs", [128, D])
    mem = din("mem", [NMEM, D])
    spool = din("spool", [2 * PST, POOLW])
    ck = din("ck", [2, NMEM, MEMW])
    cv = din("cv", [2, NMEM, MEMW])
    colv_d = din("colv", [128, NCOLV])
    wsT_d = din("wsT", [128, 4, 128])
    bsgu_d = din("bsgu", [4 * 128])
    bsgus_d = din("bsgus", [4 * 128])
    wpool_d = din("wpool", [4, 256, 256])
    gfin_d = din("gfin", [D])
    W32 = {
        "w_in": din("w_in", [D, INC]),
        "w_mk": din("w_mk", [D, MEMW]),
        "w_mv": din("w_mv", [D, MEMW]),
        "w_pa": din("w_pa", [SGUW, D]),
        "w_pb": din("w_pb", [POOLW, D]),
        "w_pc": din("w_pc", [MEMW, D]),
        "w_o": din("w_o", [D, D]),
        "w_fg": din("w_fg", [D, DFF]),
        "w_fu": din("w_fu", [D, DFF]),
        "w_fd": din("w_fd", [DFF, D]),
    }
    yp = dout("yp", [SEQ, D])
    ys = dout("ys", [128, D])
    npool_p = dout("npool_p", [PST, POOLW])
    npool_s = dout("npool_s", [2 * PST, POOLW])
    mk_o = dout("mk_o", [NMEM, MEMW])
    mv_o = dout("mv_o", [NMEM, MEMW])
    sv_o = dout("sv_o", [128, SGUW])

    NBLK = 136
    WBP = dscr("wbp", [NBLK, 128, 4096])
    blk_id = {}

    sb = nc.alloc_sbuf_tensor
    XB = [sb("x32a", [128, NS, D], F32), sb("x32b", [128, NS, D], F32)]
    XR = [["xa%d" % s for s in range(NS)], ["xb%d" % s for s in range(NS)]]
    hT = sb("hT", [128, 16, TS], BF16)
    big = sb("big", [128, NFC * TS], BF16)
    FF = sb("FF", [128, 16, TS], BF16)
    NSLOT = 6
    wring = [sb("wr%d" % i, [128, 4096], BF16) for i in range(NSLOT)]
    LX = 432
    xbc = [sb("xbc%d" % i, [128, LX], F32) for i in range(2)]
    ptA = sb("ptA", [128, LX], F32)
    ptB = sb("ptB", [128, LX], F32)
    v32 = sb("v32", [128, 512], F32)
    gtmp = [sb("gt%d" % i, [128, TS], F32) for i in range(2)]
    t2 = sb("t2", [128, TS], F32)
    pexp = sb("pexp", [128, 1024], BF16)
    pn = sb("pn", [128, 1024], BF16)
    wsTs = sb("wsTs", [128, NS, 512], BF16)
    colv = sb("colv_s", [128, NCOLV], F32)
    ident = sb("ident", [128, 128], F32)
    identb = sb("identb", [128, 128], BF16)
    wsT32 = sb("wsT32", [128, 512], F32)
    wsT32s = sb("wsT32s", [128, 512], F32)
    bsT = sb("bsT", [128, 4, 128], F32)
    bsTs = sb("bsTs", [128, 4, 128], F32)
    wpool = sb("wpool_s", [128, 4, 2, 256], BF16)
    gfin = sb("gfin_s", [128, D], F32)
    kT0 = sb("kT", [128, 8, 256], BF16)
    Vb0 = sb("Vb", [128, 2, 1024], BF16)
    prev = sb("prev", [128, 8, 3 * PST], F32)
    invc = sb("invc", [128, 4, 16], F32)
    st = sb("stats", [128, 64], F32)

    merged32 = big[:, 0:16 * TS * 2].bitcast(F32)
    F2 = big[:, 32 * TS:40 * TS]
    FFflat = FF[:].rearrange("p a b -> p (a b)")

    def ff_res(lo, hi):
        return ["FF%d" % j for j in range(lo // TS, (hi - 1) // TS + 1)]

    kbf = FFflat[:, 8 * TS:8 * TS + 2048].rearrange("p (m f) -> p m f", m=2)
    KBF = [ff_res(8 * TS, 8 * TS + 1024), ff_res(8 * TS + 1024, 8 * TS + 2048)]

    BOUNDS = {"w_in": [0, 2048, 4096, 6144, 8192, 10240], "w_pa": [0, 2048], "w_pb": [0, 2048],
              "w_pc": [0, 2048], "w_o": [0, 2048], "w_fg": [0, 1536, 3072, 4608, 5632],
              "w_fu": [0, 1536, 3072, 4608, 5632], "w_fd": [0, 1024, 2048],
              "w_mk": [0, 1024], "w_mv": [0, 1024]}
    conv_state = {"pos": None, "n": 0, "seen": {}}

    NB = 8
    banks = [nc.alloc_psum_tensor("ps%d" % i, [128, 512], F32) for i in range(NB)]
    cnt = {"b": 0, "t": 0, "w": 0, "g": 0, "x": 0}

    def bank():
        i = cnt["b"] % NB
        cnt["b"] += 1
        return banks[i], "PS:b%d" % i

    def tbank():
        bk, br = bank()
        return bk[:].bitcast(BF16), br

    def gt():
        i = cnt["g"] % 2
        cnt["g"] += 1
        return gtmp[i], "gt%d" % i

    wstate = {"t": 0}
    WB_TILE = {"w_in": 0, "w_fg": 0, "w_o": 0, "w_fu": 1, "w_fd": 1, "w_pa": 1, "w_pb": 1, "w_pc": 1,
               "w_mk": 99, "w_mv": 99}

    def wload(name, r0, nk, c0, ncol):
        assert nk * ncol <= 4096
        i = cnt["w"] % NSLOT
        cnt["w"] += 1
        slot = wring[i]
        view = slot[:, 0:nk * ncol].rearrange("p (k c) -> p k c", k=nk)
        bres = "WB:%s:%d:%d" % (name, r0, c0)
        flat = slot[:, 0:nk * ncol]
        if wstate["t"] <= WB_TILE[name]:
            src32 = W32[name][r0:r0 + 128 * nk, c0:c0 + ncol].rearrange("(k p) c -> p k c", p=128)
            P.op("pool", lambda e: e.dma_start(out=view, in_=src32),
                 writes=["W%d" % i], dma_key="ws%d" % i)
            if wstate["t"] == WB_TILE[name]:
                assert bres not in blk_id
                blk_id[bres] = len(blk_id)
                dst = WBP[blk_id[bres], :, 0:nk * ncol]
                P.op("sp", lambda e: e.dma_start(out=dst, in_=flat),
                     reads=["W%d" % i], writes=[bres], dma_key="wb%d" % i)
        else:
            src = WBP[blk_id[bres], :, 0:nk * ncol]
            P.op("sp", lambda e: e.dma_start(out=flat, in_=src),
                 reads=[bres], writes=["W%d" % i], dma_key="w%d" % i)
        return view, "W%d" % i

    P.op("pool", lambda e: e.memset(ident[:], 0.0), writes=["ident"])
    P.op("pool", lambda e: e.affine_select(out=ident[:], in_=ident[:], pattern=[[-1, 128]],
                                           compare_op=ALU.not_equal, fill=1.0, base=0,
                                           channel_multiplier=1),
         reads=["ident"], writes=["ident"])
    P.op("dve", lambda e: e.tensor_copy(out=identb[:], in_=ident[:]),
         reads=["ident"], writes=["identb"])
    P.op("sp", lambda e: e.dma_start(out=colv[:], in_=colv_d), writes=["colv"], dma_key="c0")
    P.op("sp", lambda e: e.dma_start(out=wsT32[:], in_=wsT_d.rearrange("p g i -> p (g i)")),
         writes=["wsT32"], dma_key="c1")
    P.op("sp", lambda e: e.dma_start(out=bsT[:].rearrange("p g i -> p (g i)"),
                                     in_=bsgu_d.partition_broadcast(128)),
         writes=["bsT"], dma_key="c2")
    P.op("sp", lambda e: e.dma_start(out=bsTs[:].rearrange("p g i -> p (g i)"),
                                     in_=bsgus_d.partition_broadcast(128)),
         writes=["bsTs"], dma_key="c3")
    P.op("sp", lambda e: e.dma_start(out=gfin[:], in_=gfin_d.partition_broadcast(128)),
         writes=["gfin"], dma_key="c4")

    def mask_ws(e):
        v = wsT32[:].rearrange("p (g i) -> p g i", g=4)
        return e.memset(v[64:128, :, 0:64], 0.0)
    P.op("dve", mask_ws, reads=["wsT32"], writes=["wsT32"])
    P.op("dve", lambda e: e.memset(wsT32s[:], 0.0), writes=["wsT32s"])

    def ld_wss(e):
        v = wsT32s[:].rearrange("p (g i) -> p g i", g=4)
        a = e.dma_start(out=v[0:64, :, 0:64], in_=wsT_d[0:64, :, 0:64])
        b = e.dma_start(out=v[64:128, :, 64:128], in_=wsT_d[0:64, :, 0:64])
        return [a, b]
    P.op("sp", ld_wss, writes=["wsT32s"], dma_key="c5", ndma=2)
    P.op("pool", lambda e: e.dma_start(
        out=wpool[:].rearrange("p g k c -> p (g k) c"),
        in_=wpool_d.rearrange("g (k p) c -> p (g k) c", p=128)),
        writes=["wpool"], dma_key="c6")

    def mk_inv(e):
        r = []
        for g, w in enumerate(WINS):
            for t in range(16):
                r.append(e.memset(invc[:, g, t:t + 1], 1.0 / min(t + 1, w)))
        return r
    P.op("pool", mk_inv, writes=["invc"])
    P.op("dve", lambda e: e.memset(prev[:], 0.0), writes=["prev"])

    out_ops = []

    def norm_stats(nsub, xres, xview, ssc, presummed=False):
        for s in range(0 if presummed else nsub):
            P.op("act", (lambda s: lambda e: e.activation(
                out=pexp[:, 0:1024], in_=xview(s)[:, 0:1024], func=AF.Square,
                accum_out=st[:, ssc + s:ssc + s + 1]))(s),
                reads=[xres(s)], writes=["pexp", "st%d" % (ssc + s)])
            P.op("act", (lambda s: lambda e: e.activation(
                out=pexp[:, 0:1024], in_=xview(s)[:, 1024:2048], func=AF.Square,
                accum_out=st[:, ssc + 8 + s:ssc + 8 + s + 1]))(s),
                reads=[xres(s)], writes=["pexp", "st%d" % (ssc + 8 + s)])
        r1 = ["st%d" % (ssc + s) for s in range(nsub)]
        rd = r1 + ["st%d" % (ssc + 8 + s) for s in range(nsub)]
        a0, a1 = ssc, ssc + nsub
        if presummed:
            P.op("dve", lambda e: e.tensor_reduce(
                out=st[:, a0:a1], in_=st[:, 0:8 * nsub].rearrange("p (s c) -> p s c", c=8),
                axis=AX.X, op=ALU.add),
                reads=["st%d" % j for j in range(8 * nsub)], writes=r1)
        else:
            P.op("dve", lambda e: e.tensor_tensor(out=st[:, a0:a1], in0=st[:, a0:a1],
                                                  in1=st[:, a0 + 8:a1 + 8], op=ALU.add),
                 reads=rd, writes=r1)
        P.op("dve", lambda e: e.tensor_scalar(out=st[:, a0:a1], in0=st[:, a0:a1],
                                              scalar1=1.0 / D, scalar2=EPS,
                                              op0=ALU.mult, op1=ALU.add), reads=r1, writes=r1)
        P.op("act", lambda e: e.activation(out=st[:, a0:a1], in_=st[:, a0:a1], func=AF.Sqrt),
             reads=r1, writes=r1)
        P.op("dve", lambda e: e.reciprocal(out=st[:, a0:a1], in_=st[:, a0:a1]), reads=r1, writes=r1)
        for s in range(nsub):
            xn = FFflat[:, s * 2048:(s + 1) * 2048]
            P.op("act", (lambda s, xn: lambda e: e.activation(
                out=xn, in_=xview(s), func=AF.Identity, scale=st[:, ssc + s:ssc + s + 1]))(s, xn),
                reads=[xres(s), "st%d" % (ssc + s)], writes=ff_res(s * 2048, (s + 1) * 2048))

    def norm_tr(nsub, gcol):
        for s in range(nsub):
            xn = FFflat[:, s * 2048:(s + 1) * 2048]
            xnres = ff_res(s * 2048, (s + 1) * 2048)
            for half in range(2):
                tb, tbr = tbank()

                def tr(e, xn=xn, half=half, tb=tb):
                    r = []
                    for k in range(8):
                        c = half * 8 + k
                        r.append(e.transpose(out=tb[:, k * 128:(k + 1) * 128],
                                             in_=xn[:, c * 128:(c + 1) * 128],
                                             identity=identb[:]))
                    return r
                P.op("pe", tr, reads=xnres + ["identb"], writes=[tbr])
                P.op("dve", (lambda s, half, tb: lambda e: e.tensor_tensor(
                    out=hT[:, half * 8:half * 8 + 8, s * 128:(s + 1) * 128],
                    in0=tb[:].rearrange("p (k t) -> p k t", k=8),
                    in1=colv[:, gcol + half * 8:gcol + half * 8 + 8].unsqueeze(2).to_broadcast([128, 8, 128]),
                    op=ALU.mult))(s, half, tb),
                    reads=[tbr, "colv"], writes=["hT%d_%d" % (s, half)])

    def hT_res(nsub):
        return ["hT%d_%d" % (s, h) for s in range(nsub) for h in range(2)]

    def mm_fm(wv, wr, ccol, nk, rhs_fn, rhs_res, T):
        bk, br = bank()

        def f(e):
            r = []
            for k in range(nk):
                r.append(e.matmul(bk[:, 0:T], lhsT=wv[:, k, ccol:ccol + 128], rhs=rhs_fn(k),
                                  start=(k == 0), stop=(k == nk - 1)))
            return r
        P.op("pe", f, reads=[wr] + list(rhs_res), writes=[br])
        return bk, br

    def k_transposes(src, src_res, dst, dst_res):
        for mc in range(2):
            tb, tbr = tbank()

            def tr(e, mc=mc, tb=tb):
                r = []
                for c in range(8):
                    r.append(e.transpose(out=tb[:, c * 128:(c + 1) * 128],
                                         in_=src[:, mc, c * 128:(c + 1) * 128],
                                         identity=identb[:]))
                return r
            P.op("pe", tr, reads=list(src_res[mc]) + ["identb"], writes=[tbr])
            P.op("dve", (lambda mc, tb: lambda e: e.tensor_copy(
                out=dst[:, :, mc * 128:(mc + 1) * 128],
                in_=tb[:].rearrange("p (c m) -> p c m", c=8)))(mc, tb),
                reads=[tbr], writes=[dst_res])

    def mem_phase():
        X = XB[0]
        for s in range(2):
            P.op("sp", (lambda s: lambda e: e.dma_start(out=X[:, s, :], in_=mem[s * 128:(s + 1) * 128, :]))(s),
                 writes=[XR[0][s]], dma_key="xl%d" % s)
        norm_stats(2, lambda s: XR[0][s], lambda s: X[:, s, :], 0)
        norm_tr(2, C_GMEM)
        for wi, (wname, dst) in enumerate((("w_mk", mk_o), ("w_mv", mv_o))):
            for cq in range(4):
                wv, wr = wload(wname, 0, 16, cq * 256, 256)
                for s in range(2):
                    bk, br = bank()

                    def f(e, s=s, wv=wv, bk=bk):
                        r = []
                        for k in range(16):
                            r.append(e.matmul(bk[:, 0:256], lhsT=hT[:, k, s * 128:(s + 1) * 128],
                                              rhs=wv[:, k, :], start=(k == 0), stop=(k == 15)))
                        return r
                    P.op("pe", f, reads=[wr] + hT_res(2), writes=[br])
                    g, gr = gt()
                    P.op("act", (lambda g, bk: lambda e: e.activation(out=g[:, 0:256], in_=bk[:, 0:256], func=AF.Identity))(g, bk),
                         reads=[br], writes=[gr])
                    out_ops.append(P.op("sp", (lambda dst, s, cq, g: lambda e: e.dma_start(
                        out=dst[s * 128:(s + 1) * 128, cq * 256:(cq + 1) * 256], in_=g[:, 0:256]))(dst, s, cq, g),
                        reads=[gr], dma_key="mo%d" % (cnt["x"] % 2)))
                    cnt["x"] += 1
                    if wi == 0:
                        P.op("dve", (lambda s, cq, g: lambda e: e.tensor_copy(
                            out=kbf[:, s, cq * 256:(cq + 1) * 256], in_=g[:, 0:256]))(s, cq, g),
                            reads=[gr], writes=KBF[s])
                    else:
                        P.op("dve", (lambda s, cq, g: lambda e: e.tensor_copy(
                            out=Vb0[:, s, cq * 256:(cq + 1) * 256], in_=g[:, 0:256]))(s, cq, g),
                            reads=[gr], writes=["Vb0"])
        k_transposes(kbf, KBF, kT0[:], "kT0")

    ntiles = len(plan)

    def xsrc_ap(sub):
        return xs if sub[0] == "s" else xp[sub[1]:sub[1] + 128, :]

    def ydst_ap(sub):
        return ys if sub[0] == "s" else yp[sub[1]:sub[1] + 128, :]

    def load_x(t, only=None):
        X = XB[t % 2]
        for s, sub in enumerate(plan[t]):
            if only is not None and s != only:
                continue
            P.op("sp", (lambda s, sub: lambda e: e.dma_start(out=X[:, s, :], in_=xsrc_ap(sub)))(s, sub),
                 writes=[XR[t % 2][s]], dma_key="xl%d" % s)

    def prologue_stats(t):
        X = XB[t % 2]
        norm_stats(len(plan[t]), lambda s: XR[t % 2][s], lambda s: X[:, s, :], 0)

    def prologue_tr(t):
        norm_tr(len(plan[t]), C_GMIX)

    def final_steps(t):
        X = XB[t % 2]
        xr = XR[t % 2]
        subs = plan[t]
        nsub = len(subs)
        fst = ["st%d" % (56 + j) for j in range(8)]
        steps = []

        def sq(s):
            for hh in range(2):
                P.op("act", (lambda s, hh: lambda e: e.activation(
                    out=pexp[:, 0:1024], in_=X[:, s, hh * 1024:(hh + 1) * 1024], func=AF.Square,
                    accum_out=st[:, 56 + hh * 4 + s:57 + hh * 4 + s]))(s, hh),
                    reads=[xr[s]], writes=["pexp", "st%d" % (56 + hh * 4 + s)])

        def chain():
            P.op("dve", lambda e: e.tensor_tensor(out=st[:, 56:56 + nsub], in0=st[:, 56:56 + nsub],
                                                  in1=st[:, 60:60 + nsub], op=ALU.add), reads=fst, writes=fst)
            P.op("dve", lambda e: e.tensor_scalar(out=st[:, 56:56 + nsub], in0=st[:, 56:56 + nsub],
                                                  scalar1=1.0 / D, scalar2=EPS, op0=ALU.mult, op1=ALU.add),
                 reads=fst, writes=fst)
            P.op("act", lambda e: e.activation(out=st[:, 56:56 + nsub], in_=st[:, 56:56 + nsub], func=AF.Sqrt),
                 reads=fst, writes=fst)
            P.op("dve", lambda e: e.reciprocal(out=st[:, 56:56 + nsub], in_=st[:, 56:56 + nsub]),
                 reads=fst, writes=fst)

        def yst(s):
            P.op("dve", (lambda s: lambda e: e.scalar_tensor_tensor(
                out=X[:, s, :], in0=X[:, s, :], scalar=st[:, 56 + s:57 + s], in1=gfin[:],
                op0=ALU.mult, op1=ALU.mult))(s),
                reads=[xr[s], "gfin"] + fst, writes=[xr[s]])
            out_ops.append(P.op("pool", (lambda s: lambda e: e.dma_start(
                out=ydst_ap(subs[s]), in_=X[:, s, :]))(s),
                reads=[xr[s]], dma_key="yo%d" % s))
        for s in range(nsub):
            steps.append((lambda s: lambda: sq(s))(s))
        steps.append(chain)
        for s in range(nsub):
            steps.append((lambda s: lambda: yst(s))(s))
        return steps

    def final(t):
        for f_ in final_steps(t):
            f_()

    def pool_out(c0, nrows, dst, key, stage, stage_res):
        for half in range(2):
            bk, br = bank()

            def trp(e, half=half, bk=bk):
                r = []
                for c in range(4):
                    cc = half * 4 + c
                    r.append(e.transpose(out=bk[0:nrows, c * 128:(c + 1) * 128], in_=prev[:, cc, c0:c0 + nrows],
                                         identity=ident[:]))
                return r
            P.op("pe", trp, reads=["prev", "ident"], writes=[br])
            P.op("dve", (lambda half, bk: lambda e: e.tensor_copy(
                out=stage[0:nrows, half * 512:(half + 1) * 512], in_=bk[0:nrows, :]))(half, bk),
                reads=[br], writes=[stage_res])
        out_ops.append(P.op("pool", lambda e: e.dma_start(out=dst, in_=stage[0:nrows, :]),
                            reads=[stage_res], dma_key=key))

    pending = {"final": None}

    def tile(t):
        subs = plan[t]
        nsub = len(subs)
        T = nsub * 128
        X = XB[t % 2]
        xr = XR[t % 2]
        first = (t == 0)
        last = (t == ntiles - 1)
        pl = "dve" if t <= 1 else "pool"
        kinds = [sb_[0] for sb_ in subs]
        has_s = "s" in kinds
        np_ = kinds.count("p")
        assert kinds == ["p"] * np_ + ["s"] * (nsub - np_)
        TP = np_ * 128
        hres = hT_res(nsub)
        hrhs = lambda k: hT[:, k, 0:T]

        if has_s and pending["final"] is not None:
            final(pending["final"])
            pending["final"] = None
        if has_s:
            OB = XB[(t + 1) % 2]
            obr = XR[(t + 1) % 2]
            ob16 = OB[:].rearrange("p a b -> p (a b)").bitcast(BF16)
            kTs = [ob16[:, st_ * 2048:(st_ + 1) * 2048].rearrange("p (c m) -> p c m", c=8) for st_ in range(2)]
            Vbs = [ob16[:, 4096 + st_ * 2048:4096 + (st_ + 1) * 2048].rearrange("p (c f) -> p c f", c=2) for st_ in range(2)]
            kbs = ob16[:, 8192:10240].rearrange("p (m f) -> p m f", m=2)
            spl = OB[0:32, 2, 1024:2048]
            for st_ in range(2):
                P.op("pool", (lambda st_: lambda e: e.dma_start(
                    out=kbs, in_=ck[st_].rearrange("(c p) f -> p c f", p=128)))(st_),
                    writes=[obr[2]], dma_key="kl")
                k_transposes(kbs, [[obr[2]], [obr[2]]], kTs[st_], obr[0])
                P.op("pool", (lambda st_: lambda e: e.dma_start(
                    out=Vbs[st_], in_=cv[st_].rearrange("(c p) f -> p c f", p=128)))(st_),
                    writes=[obr[1]], dma_key="vl")
            P.op("sp", lambda e: e.dma_start(out=spl[0:2 * PST, :], in_=spool), writes=[obr[2]], dma_key="c7")
            for half in range(2):
                bk, br = bank()

                def trp(e, half=half, bk=bk):
                    r = []
                    for c in range(4):
                        cc = half * 4 + c
                        r.append(e.transpose(out=bk[:, c * 32:c * 32 + 2 * PST],
                                             in_=spl[0:2 * PST, cc * 128:(cc + 1) * 128],
                                             identity=ident[0:2 * PST, 0:2 * PST]))
                    return r
                P.op("pe", trp, reads=[obr[2], "ident"], writes=[br])
                P.op("dve", (lambda half, bk: lambda e: e.tensor_copy(
                    out=prev[:, half * 4:half * 4 + 4, PST:3 * PST],
                    in_=bk[:, 0:128].rearrange("p (c t) -> p c t", c=4)[:, :, 0:2 * PST]))(half, bk),
                    reads=[br], writes=["prev"])
            kv_view = [(kT0[:], "kT0", Vb0[:], "Vb0"), (kTs[0], obr[0], Vbs[0], obr[1]), (kTs[1], obr[0], Vbs[1], obr[1])]
        else:
            kv_view = [(kT0[:], "kT0", Vb0[:], "Vb0")]

        segs = []
        b0 = 0
        if TP:
            segs.append((b0, TP, 0, 0))
            b0 += PST + TP
        if has_s:
            segs.append((b0, 64, TP, PST))
            b0 += PST + 64
            segs.append((b0, 64, TP + 64, 2 * PST))
            b0 += PST + 64
        L = b0
        assert L <= LX
        agrp = [(s * 128, 128, 0) for s in range(np_)]
        if has_s:
            agrp += [(TP, 64, 1), (TP + 64, 64, 2)]
        kvr = []
        if TP:
            kvr.append((0, TP, 0))
        if has_s:
            kvr += [(TP, 64, 1), (TP + 64, 64, 2)]

        vbase = 8 * TS
        vview = FFflat[:, vbase:vbase + nsub * 1024]
        for q in range(4):
            wv, wr = wload("w_in", 0, 16, OFF_V + q * 256, 256)
            for s in range(nsub):
                bk, br = bank()
                vb = v32[:, (cnt["x"] % 2) * 256:(cnt["x"] % 2) * 256 + 256]
                vbr = "v32_%d" % (cnt["x"] % 2)
                cnt["x"] += 1

                def f(e, s=s, wv=wv, bk=bk):
                    r = []
                    for k in range(16):
                        r.append(e.matmul(bk[:, 0:256], lhsT=hT[:, k, s * 128:(s + 1) * 128],
                                          rhs=wv[:, k, :], start=(k == 0), stop=(k == 15)))
                    return r
                P.op("pe", f, reads=[wr] + hres, writes=[br])
                P.op("act", (lambda bk, vb: lambda e: e.activation(out=vb, in_=bk[:, 0:256], func=AF.Gelu_apprx_tanh))(bk, vb),
                     reads=[br], writes=[vbr])
                if kinds[s] == "s":
                    out_ops.append(P.op("pool", (lambda q, vb: lambda e: e.dma_start(
                        out=sv_o[:, q * 256:(q + 1) * 256], in_=vb))(q, vb),
                        reads=[vbr], dma_key="svo"))
                P.op("act", (lambda s, q, vb: lambda e: e.activation(
                    out=pexp[:, 0:256], in_=vb, func=AF.Square,
                    accum_out=st[:, 16 + q * 4 + s:16 + q * 4 + s + 1]))(s, q, vb),
                    reads=[vbr], writes=["pexp", "st%d" % (16 + q * 4 + s)])
                lo = vbase + s * 1024 + q * 256
                P.op("dve", (lambda s, q, vb: lambda e: e.tensor_copy(
                    out=vview[:, s * 1024 + q * 256:s * 1024 + (q + 1) * 256], in_=vb))(s, q, vb),
                    reads=[vbr], writes=ff_res(lo, lo + 256))
        def v_stats_chain():
            vst = ["st%d" % (16 + j) for j in range(16)]
            for j in (1, 2, 3):
                P.op("dve", (lambda j: lambda e: e.tensor_tensor(
                    out=st[:, 16:16 + nsub], in0=st[:, 16:16 + nsub],
                    in1=st[:, 16 + 4 * j:16 + 4 * j + nsub], op=ALU.add))(j),
                    reads=vst, writes=vst)
            P.op("dve", lambda e: e.tensor_scalar(out=st[:, 16:16 + nsub], in0=st[:, 16:16 + nsub],
                                                  scalar1=1.0 / SGUW, scalar2=EPS, op0=ALU.mult, op1=ALU.add),
                 reads=vst, writes=vst)
            P.op("act", lambda e: e.activation(out=st[:, 16:16 + nsub], in_=st[:, 16:16 + nsub], func=AF.Sqrt),
                 reads=vst, writes=vst)
            P.op("dve", lambda e: e.reciprocal(out=st[:, 16:16 + nsub], in_=st[:, 16:16 + nsub]),
                 reads=vst, writes=vst)
            for s in range(nsub):
                wsrc = wsT32s if kinds[s] == "s" else wsT32
                P.op("dve", (lambda s, wsrc: lambda e: e.tensor_scalar(
                    out=wsTs[:, s, :], in0=wsrc[:], scalar1=st[:, 16 + s:17 + s], scalar2=None,
                    op0=ALU.mult))(s, wsrc),
                    reads=vst + ["wsT32", "wsT32s"], writes=["wsTs%d" % s])
        for q in range(4):
            if q == 2:
                v_stats_chain()
            wv, wr = wload("w_in", 0, 16, OFF_U + q * 256, 256)
            for c4 in range(2):
                fc = q * 2 + c4
                bk, br = mm_fm(wv, wr, c4 * 128, 16, hrhs, hres, T)
                P.op("act", (lambda fc, bk: lambda e: e.activation(
                    out=FF[:, fc, 0:T], in_=bk[:, 0:T], func=AF.Gelu_apprx_tanh))(fc, bk),
                    reads=[br], writes=["FF%d" % fc])
        fsteps = []
        if pending["final"] is not None:
            fsteps = final_steps(pending["final"])
            pending["final"] = None
        vres_all = ff_res(vbase, vbase + nsub * 1024)
        for fc in range(8):
            g = fc // 2
            bk, br = bank()

            def f(e, fc=fc, g=g, bk=bk):
                r = []
                for s in range(nsub):
                    r.append(e.matmul(bk[:, s * 128:(s + 1) * 128],
                                      lhsT=vview[:, s * 1024 + fc * 128:s * 1024 + (fc + 1) * 128],
                                      rhs=wsTs[:, s, g * 128:(g + 1) * 128], start=True, stop=True))
                return r
            P.op("pe", f, reads=vres_all + ["wsTs%d" % s for s in range(nsub)], writes=[br])

            def comb(e, fc=fc, g=g, bk=bk):
                r = []
                if np_:
                    r.append(e.scalar_tensor_tensor(
                        out=t2[:, 0:TP].rearrange("p (s i) -> p s i", s=np_),
                        in0=bk[:, 0:TP].rearrange("p (s i) -> p s i", s=np_),
                        scalar=colv[:, C_GV + fc:C_GV + fc + 1],
                        in1=bsT[:, g:g + 1, :].to_broadcast([128, np_, 128]),
                        op0=ALU.mult, op1=ALU.add))
                if has_s:
                    r.append(e.scalar_tensor_tensor(
                        out=t2[:, TP:T], in0=bk[:, TP:T],
                        scalar=colv[:, C_GV + fc:C_GV + fc + 1],
                        in1=bsTs[:, g, :], op0=ALU.mult, op1=ALU.add))
                return r
            P.op("dve", comb, reads=[br, "colv", "bsT", "bsTs"], writes=["t2"])
            P.op("dve", (lambda fc: lambda e: e.tensor_tensor(
                out=FF[:, fc, 0:T], in0=FF[:, fc, 0:T], in1=t2[:, 0:T], op=ALU.mult))(fc),
                reads=["t2", "FF%d" % fc], writes=["FF%d" % fc])

        bblk = {}

        def b_chunk(c):
            q, c4 = c // 2, c % 2
            if c4 == 0:
                bblk["w"] = wload("w_in", 0, 16, OFF_B + q * 256, 256)
            wv, wr = bblk["w"]
            g = c // 2
            w = WINS[g]
            bk, br = mm_fm(wv, wr, c4 * 128, 16, hrhs, hres, T)
            xb = xbc[c % 2]
            xbr = "xbc%d" % (c % 2)

            def cp_act(e, c=c, bk=bk, xb=xb):
                r = []
                for (sb0, nt, pc0, pv0) in segs:
                    r.append(e.activation(out=xb[:, sb0:sb0 + PST], in_=prev[:, c, pv0:pv0 + PST],
                                          func=AF.Identity))
                    r.append(e.activation(out=xb[:, sb0 + PST:sb0 + PST + nt],
                                          in_=bk[:, pc0:pc0 + nt], func=AF.Identity))
                return r
            P.op("act", cp_act, reads=[br, "prev"], writes=[xbr])

            def sv(e, c=c, xb=xb):
                r = []
                for (sb0, nt, pc0, pv0) in segs:
                    r.append(e.tensor_copy(out=prev[:, c, pv0:pv0 + PST],
                                           in_=xb[:, sb0 + nt:sb0 + nt + PST]))
                return r
            P.op("dve", sv, reads=[xbr], writes=["prev"])
            P.op(pl, (lambda xb: lambda e: e.tensor_tensor(
                out=ptA[:, 1:L], in0=xb[:, 1:L], in1=xb[:, 0:L - 1], op=ALU.add))(xb),
                reads=[xbr], writes=["ptA"])
            if w >= 4:
                P.op(pl, lambda e: e.tensor_tensor(
                    out=ptB[:, 3:L], in0=ptA[:, 3:L], in1=ptA[:, 1:L - 2], op=ALU.add),
                    reads=["ptA"], writes=["ptB"])
            if w >= 8:
                P.op(pl, lambda e: e.tensor_tensor(
                    out=ptA[:, 7:L], in0=ptB[:, 7:L], in1=ptB[:, 3:L - 4], op=ALU.add),
                    reads=["ptB"], writes=["ptA"])
            if w >= 16:
                P.op(pl, lambda e: e.tensor_tensor(
                    out=ptB[:, 15:L], in0=ptA[:, 15:L], in1=ptA[:, 7:L - 8], op=ALU.add),
                    reads=["ptA"], writes=["ptB"])
            sres = ptB if w in (4, 16) else ptA

            def fin(e, c=c, g=g, w=w, xb=xb, sres=sres):
                r = []
                for (sb0, nt, pc0, pv0) in segs:
                    r.append(e.scalar_tensor_tensor(
                        out=F2[:, c * TS + pc0:c * TS + pc0 + nt], in0=sres[:, sb0 + PST:sb0 + PST + nt],
                        scalar=1.0 / w, in1=xb[:, sb0 + PST:sb0 + PST + nt],
                        op0=ALU.mult, op1=ALU.subtract))
                return r
            P.op("dve", fin, reads=["ptA", "ptB", xbr], writes=["big%d" % (32 + c)])
            if first:
                P.op("dve", (lambda g, sres: lambda e: e.tensor_tensor(
                    out=t2[:, 0:PST], in0=sres[:, PST:2 * PST], in1=invc[:, g, 0:PST], op=ALU.mult))(g, sres),
                    reads=["ptA", "ptB", "invc"], writes=["t2"])
                P.op("dve", (lambda c, xb: lambda e: e.tensor_tensor(
                    out=F2[:, c * TS:c * TS + PST], in0=t2[:, 0:PST], in1=xb[:, PST:2 * PST], op=ALU.subtract))(c, xb),
                    reads=["t2", xbr], writes=["big%d" % (32 + c)])
        def proj_gate(bi, wname, src_fn, src_res, extra=None):
            for q in range(4):
                pv, pr = wload(wname, 0, 8, q * 512, 512)
                gblk = [None, None]
                for c4 in range(4):
                    fc = q * 4 + c4
                    if c4 % 2 == 0:
                        h2 = c4 // 2
                        gblk[h2] = wload("w_in", 0, 16, OFF_G + bi * D + q * 512 + h2 * 256, 256)
                    if extra is not None:
                        extra(fc)
                    gv_, gr_ = gblk[c4 // 2]
                    yb_, ybr = mm_fm(pv, pr, c4 * 128, 8, src_fn, src_res, T)
                    gb_, gbr = mm_fm(gv_, gr_, (c4 % 2) * 128, 16, hrhs, hres, T)
                    g, gr = gt()
                    P.op("act", (lambda fc, gb_, g: lambda e: e.activation(
                        out=g[:, 0:T], in_=gb_[:, 0:T], func=AF.Sigmoid,
                        bias=colv[:, C_BG + bi * 16 + fc:C_BG + bi * 16 + fc + 1]))(fc, gb_, g),
                        reads=[gbr, "colv"], writes=[gr])
                    mres = ["big%d" % (2 * fc), "big%d" % (2 * fc + 1)]
                    m32 = merged32[:, fc * TS:fc * TS + T]
                    if bi == 0:
                        P.op("dve", (lambda g, yb_, m32: lambda e: e.tensor_tensor(
                            out=m32, in0=g[:, 0:T], in1=yb_[:, 0:T], op=ALU.mult))(g, yb_, m32),
                            reads=[gr, ybr], writes=mres)
                    elif bi == 1:
                        P.op("dve", (lambda g, yb_: lambda e: e.tensor_tensor(
                            out=g[:, 0:T], in0=g[:, 0:T], in1=yb_[:, 0:T], op=ALU.mult))(g, yb_),
                            reads=[gr, ybr], writes=[gr])
                        P.op(pl, (lambda g, m32: lambda e: e.tensor_tensor(
                            out=m32, in0=m32, in1=g[:, 0:T], op=ALU.add))(g, m32),
                            reads=[gr] + mres, writes=mres)
                    else:
                        P.op("dve", (lambda g, yb_: lambda e: e.tensor_tensor(
                            out=g[:, 0:T], in0=g[:, 0:T], in1=yb_[:, 0:T], op=ALU.mult))(g, yb_),
                            reads=[gr, ybr], writes=[gr])
                        P.op(pl, (lambda fc, g, m32: lambda e: e.tensor_tensor(
                            out=FF[:, fc, 0:T], in0=m32, in1=g[:, 0:T], op=ALU.add))(fc, g, m32),
                            reads=[gr] + mres, writes=["FF%d" % fc])

        nfs = len(fsteps)
        fsched = {}
        if nfs:
            nsq = (nfs - 1) // 2
            for i_ in range(nsq + 1):
                fsched[i_] = i_
            for i_ in range(nsq):
                fsched[5 + 4 * i_] = nsq + 1 + i_
        fdone = set()

        def extra_a(fc):
            if fc % 2 == 0:
                b_chunk(fc // 2)
            if fc in fsched and fsched[fc] < nfs:
                fsteps[fsched[fc]]()
                fdone.add(fsched[fc])
        proj_gate(0, "w_pa", lambda k: FF[:, k, 0:T], ["FF%d" % k for k in range(8)], extra=extra_a)
        for i_ in range(nfs):
            if i_ not in fdone:
                fsteps[i_]()

        for g in range(4):
            for dc in range(2):
                bk, br = bank()

                def f(e, g=g, dc=dc, bk=bk):
                    r = []
                    for cc in range(2):
                        r.append(e.matmul(bk[:, 0:T], lhsT=wpool[:, g, cc, dc * 128:(dc + 1) * 128],
                                          rhs=F2[:, (2 * g + cc) * TS:(2 * g + cc) * TS + T], start=(cc == 0), stop=(cc == 1)))
                    return r
                P.op("pe", f, reads=["wpool", "big%d" % (32 + 2 * g), "big%d" % (33 + 2 * g)], writes=[br])
                c = 2 * g + dc
                P.op("dve", (lambda c, bk: lambda e: e.tensor_scalar(
                    out=FF[:, 8 + c, 0:T], in0=bk[:, 0:T],
                    scalar1=colv[:, C_PSC + c:C_PSC + c + 1], scalar2=None, op0=ALU.mult))(c, bk),
                    reads=[br, "colv"], writes=["FF%d" % (8 + c)])
        proj_gate(1, "w_pb", lambda k: FF[:, 8 + k, 0:T], ["FF%d" % (8 + k) for k in range(8)])

        for q in range(4):
            wv, wr = wload("w_in", 0, 16, OFF_Q + q * 256, 256)
            for c4 in range(2):
                c = q * 2 + c4
                bk, br = mm_fm(wv, wr, c4 * 128, 16, hrhs, hres, T)
                P.op("act", (lambda c, bk: lambda e: e.activation(
                    out=FF[:, c, 0:T], in_=bk[:, 0:T], func=AF.Identity, scale=0.0625))(c, bk),
                    reads=[br], writes=["FF%d" % c])
        for bi0 in range(0, len(agrp), 3):
            batch = agrp[bi0:bi0 + 3]
            scb = []
            for (c0, tn, kvi) in batch:
                kTv, kTr, _, _ = kv_view[kvi]
                b0_, b0r = bank()
                b1_, b1r = bank()

                def sc(e, c0=c0, tn=tn, kTv=kTv, b0_=b0_, b1_=b1_):
                    r = []
                    for h in range(4):
                        bk = b0_ if h < 2 else b1_
                        hh = h % 2
                        for dc in range(2):
                            r.append(e.matmul(bk[0:tn, hh * 256:(hh + 1) * 256],
                                              lhsT=FF[:, h * 2 + dc, c0:c0 + tn],
                                              rhs=kTv[:, h * 2 + dc, :], start=(dc == 0), stop=(dc == 1)))
                    return r
                P.op("pe", sc, reads=["FF%d" % k for k in range(8)] + [kTr], writes=[b0r, b1r])
                scb.append((b0_, b0r, b1_, b1r))
            v32b = v32[:].bitcast(BF16)
            PBUF = [(pexp, ["pexp"]), (pn, ["pn"]), (v32b, ["v32_0", "v32_1"])]
            MXC = [36, 32, 3]
            SMC = [40, 47, 11]
            for j, (c0, tn, kvi) in enumerate(batch):
                b0_, b0r, b1_, b1r = scb[j]
                mxc = MXC[j]
                mxr = "mx%d" % j
                P.op("dve", (lambda tn, b0_, mxc: lambda e: e.tensor_reduce(
                    out=st[0:tn, mxc:mxc + 2], in_=b0_[0:tn, :].rearrange("p (h m) -> p h m", h=2), axis=AX.X, op=ALU.max))(tn, b0_, mxc),
                    reads=[b0r], writes=[mxr + "a"])
                P.op("dve", (lambda tn, b1_, mxc: lambda e: e.tensor_reduce(
                    out=st[0:tn, mxc + 2:mxc + 4], in_=b1_[0:tn, :].rearrange("p (h m) -> p h m", h=2), axis=AX.X, op=ALU.max))(tn, b1_, mxc),
                    reads=[b1r], writes=[mxr + "b"])
                P.op("dve", (lambda tn, mxc: lambda e: e.tensor_scalar(out=st[0:tn, mxc:mxc + 4], in0=st[0:tn, mxc:mxc + 4], scalar1=-1.0,
                                                                    scalar2=None, op0=ALU.mult))(tn, mxc),
                     reads=[mxr + "a", mxr + "b"], writes=[mxr + "a", mxr + "b"])
            for j, (c0, tn, kvi) in enumerate(batch):
                b0_, b0r, b1_, b1r = scb[j]
                pb, pbr = PBUF[j]
                mxc, smc = MXC[j], SMC[j]
                mxr, smr = "mx%d" % j, "sm%d" % j

                def ex(e, tn=tn, b0_=b0_, b1_=b1_, pb=pb, mxc=mxc, smc=smc):
                    r = []
                    for h in range(4):
                        bk = b0_ if h < 2 else b1_
                        hh = h % 2
                        r.append(e.activation(out=pb[0:tn, h * 256:(h + 1) * 256],
                                              in_=bk[0:tn, hh * 256:(hh + 1) * 256], func=AF.Exp,
                                              bias=st[0:tn, mxc + h:mxc + h + 1], accum_out=st[0:tn, smc + h:smc + h + 1]))
                    return r
                P.op("act", ex, reads=[b0r, b1r, mxr + "a", mxr + "b"], writes=pbr + [smr])
            for j, (c0, tn, kvi) in enumerate(batch):
                pb, pbr = PBUF[j]
                smc = SMC[j]
                smr = "sm%d" % j
                P.op("dve", (lambda tn, smc: lambda e: e.reciprocal(out=st[0:tn, smc:smc + 4], in_=st[0:tn, smc:smc + 4]))(tn, smc),
                     reads=[smr], writes=[smr])
                P.op("dve", (lambda tn, pb, smc: lambda e: e.tensor_tensor(
                    out=pb[0:tn, :].rearrange("p (h m) -> p h m", h=4),
                    in0=pb[0:tn, :].rearrange("p (h m) -> p h m", h=4),
                    in1=st[0:tn, smc:smc + 4].unsqueeze(2).to_broadcast([tn, 4, 256]), op=ALU.mult))(tn, pb, smc),
                    reads=pbr + [smr], writes=pbr)
            tbs = []
            for j, (c0, tn, kvi) in enumerate(batch):
                pb, pbr = PBUF[j]
                tb, tbr = tbank()

                def trp(e, tn=tn, tb=tb, pb=pb):
                    r = []
                    for jj in range(8):
                        r.append(e.transpose(out=tb[:, jj * 128:jj * 128 + tn], in_=pb[0:tn, jj * 128:(jj + 1) * 128],
                                             identity=identb[0:tn, 0:tn]))
                    return r
                P.op("pe", trp, reads=pbr + ["identb"], writes=[tbr])
                tbs.append((tb, tbr))

                def emit_copy(jx):
                    c0x, tnx, _ = batch[jx]
                    tbx, tbrx = tbs[jx]
                    P.op("dve", (lambda c0x, tnx, tbx: lambda e: e.tensor_copy(
                        out=FF[:, 8:16, c0x:c0x + tnx],
                        in_=tbx[:].rearrange("p (j t) -> p j t", j=8)[:, :, 0:tnx]))(c0x, tnx, tbx),
                        reads=[tbrx], writes=["FF%d" % (8 + jj) for jj in range(8)])
                if j >= 1:
                    emit_copy(j - 1)
            emit_copy(len(batch) - 1)
        for h in range(4):
            for dc in range(2):
                bk, br = bank()

                def f(e, h=h, dc=dc, bk=bk):
                    r = []
                    for (c0, ncl, kvi) in kvr:
                        Vv = kv_view[kvi][2]
                        for mc in range(2):
                            r.append(e.matmul(bk[:, c0:c0 + ncl],
                                              lhsT=Vv[:, mc, h * 256 + dc * 128:h * 256 + (dc + 1) * 128],
                                              rhs=FF[:, 8 + h * 2 + mc, c0:c0 + ncl], start=(mc == 0), stop=(mc == 1)))
                    return r
                P.op("pe", f, reads=list({kv_view[kvi][3] for (_, _, kvi) in kvr}) + ["FF%d" % (8 + h * 2), "FF%d" % (9 + h * 2)],
                     writes=[br])
                c = h * 2 + dc
                P.op("act", (lambda c, bk: lambda e: e.activation(
                    out=F2[:, c * TS:c * TS + T], in_=bk[:, 0:T], func=AF.Identity))(c, bk),
                    reads=[br], writes=["big%d" % (32 + c)])
        proj_gate(2, "w_pc", lambda k: F2[:, k * TS:k * TS + T], ["big%d" % (32 + k) for k in range(8)])

        for cq in range(8):
            wv, wr = wload("w_o", 0, 16, cq * 256, 256)
            for s in range(nsub):
                bk, br = bank()

                def f(e, s=s, wv=wv, bk=bk):
                    r = []
                    for k in range(16):
                        r.append(e.matmul(bk[:, 0:256], lhsT=FF[:, k, s * 128:(s + 1) * 128], rhs=wv[:, k, :],
                                          start=(k == 0), stop=(k == 15)))
                    return r
                P.op("pe", f, reads=[wr] + ["FF%d" % k for k in range(16)], writes=[br])
                P.op("dve", (lambda s, cq, bk: lambda e: e.tensor_tensor(
                    out=X[:, s, cq * 256:(cq + 1) * 256], in0=X[:, s, cq * 256:(cq + 1) * 256],
                    in1=bk[:, 0:256], op=ALU.add))(s, cq, bk),
                    reads=[br, xr[s]], writes=[xr[s]])
                P.op("act", (lambda s, cq: lambda e: e.activation(
                    out=pexp[:, 0:256], in_=X[:, s, cq * 256:(cq + 1) * 256], func=AF.Square,
                    accum_out=st[:, s * 8 + cq:s * 8 + cq + 1]))(s, cq),
                    reads=[xr[s]], writes=["pexp", "st%d" % (s * 8 + cq)])

        norm_stats(nsub, lambda s: xr[s], lambda s: X[:, s, :], 44, presummed=True)
        norm_tr(nsub, C_GFFN)
        for q in range(NFC // 2):
            if not last and q % 7 == 1 and q // 7 < len(plan[t + 1]):
                load_x(t + 1, only=q // 7)
            gv_, gr_ = wload("w_fg", 0, 16, q * 256, 256)
            uv_, ur_ = wload("w_fu", 0, 16, q * 256, 256)
            for c4 in range(2):
                fc = q * 2 + c4
                gb_, gbr = mm_fm(gv_, gr_, c4 * 128, 16, hrhs, hres, T)
                ub_, ubr = mm_fm(uv_, ur_, c4 * 128, 16, hrhs, hres, T)
                g, gr = gt()
                P.op("act", (lambda gb_, g: lambda e: e.activation(out=g[:, 0:T], in_=gb_[:, 0:T], func=AF.Silu))(gb_, g),
                     reads=[gbr], writes=[gr])
                P.op("dve", (lambda fc, g, ub_: lambda e: e.tensor_tensor(
                    out=big[:, fc * TS:fc * TS + T], in0=g[:, 0:T], in1=ub_[:, 0:T], op=ALU.mult))(fc, g, ub_),
                    reads=[gr, ubr], writes=["big%d" % fc])
        if not last:
            prologue_stats(t + 1)
        for cq in range(8):
            if cq == 4 and not last:
                prologue_tr(t + 1)
            bks = [bank() for _ in range(nsub)]
            for part in range(4):
                wv, wr = wload("w_fd", part * 11 * 128, 11, cq * 256, 256)
                for s in range(nsub):
                    bk, br = bks[s]

                    def f(e, s=s, part=part, wv=wv, bk=bk):
                        r = []
                        for k in range(11):
                            kk = part * 11 + k
                            r.append(e.matmul(bk[:, 0:256], lhsT=big[:, kk * TS + s * 128:kk * TS + (s + 1) * 128],
                                              rhs=wv[:, k, :], start=(kk == 0), stop=(kk == NFC - 1)))
                        return r
                    P.op("pe", f, reads=[wr] + ["big%d" % (part * 11 + k) for k in range(11)], writes=[br])
            for s in range(nsub):
                bk, br = bks[s]
                P.op("dve", (lambda s, cq, bk: lambda e: e.tensor_tensor(
                    out=X[:, s, cq * 256:(cq + 1) * 256], in0=X[:, s, cq * 256:(cq + 1) * 256],
                    in1=bk[:, 0:256], op=ALU.add))(s, cq, bk),
                    reads=[br, xr[s]], writes=[xr[s]])
        pending["final"] = t

    mem_phase()
    load_x(0)
    prologue_stats(0)
    prologue_tr(0)
    for t in range(ntiles):
        wstate["t"] = t
        tile(t)
    final(pending["final"])
    lastb = XB[(ntiles - 1) % 2]
    stg = XB[ntiles % 2]
    stg_r = XR[ntiles % 2]
    pool_out(0, PST, npool_p, "npo_p", stg[0:32, 0, 0:1024], stg_r[0])
    if any(sub[0] == "s" for tl in plan for sub in tl):
        pool_out(PST, 2 * PST, npool_s, "npo_s", stg[0:32, 0, 1024:2048], stg_r[0])
    P.finish_on("pool", out_ops)
    P.emit()
    return nc


_NC_CACHE = {}


def _get_nc():
    if "nc" not in _NC_CACHE:
        _NC_CACHE["nc"] = build_program()
    return _NC_CACHE["nc"]


def make_in_maps(inp):
    f = lambda a: np.ascontiguousarray(np.asarray(a, dtype=np.float32))
    colv = np.concatenate([f(inp["g_mix"])[0], f(inp["g_ffn"])[0], f(inp["b_gate"])[0],
                           f(inp["pool_scale"])[0], f(inp["g_sgu_v"])[0], f(inp["g_mem"])[0]])
    colv = np.ascontiguousarray(colv.reshape(NCOLV, 128).T)
    w_sgu = f(inp["w_sgu"])[0]
    wsT = np.ascontiguousarray(np.transpose(w_sgu, (2, 0, 1)))
    b_sgu = f(inp["b_sgu"])[0]
    bsgus = np.ascontiguousarray(np.concatenate([b_sgu[:, :64], b_sgu[:, :64]], axis=1)).reshape(-1)
    shared = {
        "colv": colv, "wsT": wsT, "bsgu": np.ascontiguousarray(b_sgu.reshape(-1)), "bsgus": bsgus,
        "wpool": f(inp["w_pool"])[0], "gfin": f(inp["g_final"]),
        "w_in": f(inp["w_in"])[0], "w_mk": f(inp["w_mk"])[0], "w_mv": f(inp["w_mv"])[0],
        "w_pa": f(inp["w_pa"])[0], "w_pb": f(inp["w_pb"])[0], "w_pc": f(inp["w_pc"])[0],
        "w_o": f(inp["w_o"])[0], "w_fg": f(inp["w_ff_gate"])[0], "w_fu": f(inp["w_ff_up"])[0],
        "w_fd": f(inp["w_ff_down"])[0],
    }
    xp, xs, mem = f(inp["x_prompt"]), f(inp["x_sample"]), f(inp["mem_prompt"])
    sp, ck, cv = f(inp["state_pool"])[0], f(inp["cache_mem_k"])[0], f(inp["cache_mem_v"])[0]
    maps = []
    for c in range(NCORE):
        m = dict(shared)
        m["xp"] = xp[c]
        m["xs"] = np.ascontiguousarray(xs[2 * c:2 * c + 2].reshape(128, D))
        m["mem"] = mem[c]
        m["spool"] = np.ascontiguousarray(sp[2 * c:2 * c + 2].reshape(2 * PST, POOLW))
        m["ck"] = np.ascontiguousarray(ck[2 * c:2 * c + 2].reshape(2, NMEM, MEMW))
        m["cv"] = np.ascontiguousarray(cv[2 * c:2 * c + 2].reshape(2, NMEM, MEMW))
        maps.append(m)
    return maps


def kernel(**inp):
    nc = _get_nc()
    maps = make_in_maps(inp)
    res = run_bass_kernel_spmd(nc, maps, core_ids=list(range(NCORE)))
    R = res.results
    y_prompt = np.stack([R[c]["yp"] for c in range(NCORE)]).astype(np.float32)
    y_sample = np.concatenate([R[c]["ys"].reshape(2, 64, D) for c in range(NCORE)]).astype(np.float32)
    npp = np.stack([R[c]["npool_p"] for c in range(NCORE)])[None].astype(np.float32)
    nps = np.concatenate([R[c]["npool_s"].reshape(2, PST, POOLW) for c in range(NCORE)])[None].astype(np.float32)
    mk = np.stack([R[c]["mk_o"].reshape(NMEM, 4, 256) for c in range(NCORE)])[None].astype(np.float32)
    mv = np.stack([R[c]["mv_o"].reshape(NMEM, 4, 256) for c in range(NCORE)])[None].astype(np.float32)
    sv = np.concatenate([R[c]["sv_o"].reshape(2, 64, SGUW) for c in range(NCORE)])[None].astype(np.float32)
    return (y_prompt, y_sample, npp, nps, mk, mv, sv)
```

```python
import numpy as np
import concourse.bass as bass
import concourse.mybir as mybir
from concourse.bass_utils import run_bass_kernel_spmd

F32 = mybir.dt.float32
BF16 = mybir.dt.bfloat16
AF = mybir.ActivationFunctionType
ALU = mybir.AluOpType
AX = mybir.AxisListType

D = 2048
SEQ = 4096
NCORE = 8
SGUW = 1024
POOLW = 1024
MEMW = 1024
NMEM = 256
DFF = 5632
INC = 10240
OFF_U, OFF_V, OFF_B, OFF_Q, OFF_G = 0, 1024, 2048, 3072, 4096
EPS = 1e-6
WINS = (2, 4, 8, 16)
PST = 15
NFC = DFF // 128

ENGS = ("pe", "act", "dve", "pool", "sp")


class Op:
    __slots__ = ("eng", "fn", "deps", "idx", "is_dma", "key", "flag", "count",
                 "ndma", "name")

    def __init__(self, eng, fn, is_dma, key, name):
        self.eng = eng
        self.fn = fn
        self.is_dma = is_dma
        self.key = key
        self.deps = []
        self.flag = False
        self.count = None
        self.ndma = 1
        self.name = name


class Prog:
    def __init__(self, nc):
        self.nc = nc
        self.ops = {e: [] for e in ENGS}
        self.last_w = {}
        self.readers = {}
        self.nops = 0
        self.last_dma_on_key = {}
        self.dma_keys = []
        self.known = {e: {} for e in ENGS}

    def op(self, eng, fn, reads=(), writes=(), dma_key=None, name="", ndma=1):
        is_dma = dma_key is not None
        o = Op(eng, fn, is_dma, dma_key, name)
        o.ndma = ndma
        o.idx = self.nops
        self.nops += 1
        deps = {}
        extra = [r + "#x" for r in reads if r.startswith("PS:")]
        if extra:
            writes = list(writes) + extra
        for r in reads:
            w = self.last_w.get(r)
            if w is not None:
                deps[w.idx] = w
        for r in writes:
            w = self.last_w.get(r)
            if w is not None:
                deps[w.idx] = w
            for rd in self.readers.get(r, ()):
                deps[rd.idx] = rd
        if is_dma:
            if dma_key not in self.last_dma_on_key:
                self.dma_keys.append(dma_key)
            prev = self.last_dma_on_key.get(dma_key)
            if prev is not None:
                deps[prev.idx] = prev
            self.last_dma_on_key[dma_key] = o
        kn = self.known[eng]
        for d in sorted(deps.values(), key=lambda x: x.idx):
            src = ("dma", d.key) if d.is_dma else ("eng", d.eng)
            if kn.get(src, -1) >= d.idx:
                continue
            kn[src] = d.idx
            o.deps.append(d)
            d.flag = True
        for r in reads:
            self.readers.setdefault(r, []).append(o)
        for r in writes:
            self.last_w[r] = o
            self.readers[r] = []
        self.ops[eng].append(o)
        return o

    def finish_on(self, eng, ops):
        o = self.op(eng, None, name="final")
        for d in ops:
            o.deps.append(d)
            d.flag = True
        return o

    def emit(self):
        nc = self.nc
        sems = {}
        for e in ("pe", "act", "dve", "pool"):
            sems[("eng", e)] = nc.alloc_semaphore("s_" + e)
        for k in self.dma_keys:
            sems[("dma", k)] = nc.alloc_semaphore("d_" + str(k))
        for e in ENGS:
            c = 0
            for o in self.ops[e]:
                if o.is_dma:
                    continue
                if o.flag:
                    c += 1
                    o.count = c
        dcount = {k: 0 for k in self.dma_keys}
        allops = []
        for e in ENGS:
            allops.extend(self.ops[e])
        allops.sort(key=lambda x: x.idx)
        for o in allops:
            if o.is_dma:
                dcount[o.key] += 16 * max(1, o.ndma)
                o.count = dcount[o.key]
        prog = self

        def run(engname, eng):
            for o in prog.ops[engname]:
                for d in o.deps:
                    if d.is_dma:
                        eng.wait_ge(sems[("dma", d.key)], d.count)
                    else:
                        eng.wait_ge(sems[("eng", d.eng)], d.count)
                if o.fn is None:
                    continue
                insts = o.fn(eng)
                if insts is None:
                    insts = []
                elif not isinstance(insts, (list, tuple)):
                    insts = [insts]
                if o.is_dma:
                    assert len(insts) == o.ndma, (o.name, len(insts), o.ndma)
                    for i in insts:
                        i.then_inc(sems[("dma", o.key)], 16)
                elif o.flag:
                    insts[-1].then_inc(sems[("eng", engname)], 1)

        with nc.Block() as block:
            @block.tensor
            def _(eng):
                run("pe", eng)

            @block.scalar
            def _(eng):
                run("act", eng)

            @block.vector
            def _(eng):
                run("dve", eng)

            @block.gpsimd
            def _(eng):
                run("pool", eng)

            @block.sync
            def _(eng):
                run("sp", eng)


C_GMIX, C_GFFN, C_BG, C_PSC, C_GV, C_GMEM, NCOLV = 0, 16, 32, 80, 88, 96, 112
TS = 384
NS = TS // 128


def default_plan():
    plan = []
    for t in range(10):
        plan.append([("p", t * TS + s * 128) for s in range(NS)])
    plan.append([("p", 3840), ("p", 3968), ("s", 0)])
    return plan


def build_program(plan=None):
    if plan is None:
        plan = default_plan()
    nc = bass.Bass("TRN2", target_bir_lowering=False)
    P = Prog(nc)

    def din(name, shape, dt=F32):
        return nc.dram_tensor(name, list(shape), dt, kind="ExternalInput").ap()

    def dout(name, shape):
        return nc.dram_tensor(name, list(shape), F32, kind="ExternalOutput").ap()

    def dscr(name, shape):
        return nc.dram_tensor(name, list(shape), BF16, kind="Internal").ap()

    xp = din("xp", [SEQ, D])
    xs = din("xs", [128, D])
    mem = din("mem", [NMEM, D])
    spool = din("spool", [2 * PST, POOLW])
    ck = din("ck", [2, NMEM, MEMW])
    cv = din("cv", [2, NMEM, MEMW])
    colv_d = din("colv", [128, NCOLV])
    wsT_d = din("wsT", [128, 4, 128])
    bsgu_d = din("bsgu", [4 * 128])
    bsgus_d = din("bsgus", [4 * 128])
    wpool_d = din("wpool", [4, 256, 256])
    gfin_d = din("gfin", [D])
    W32 = {
        "w_in": din("w_in", [D, INC]),
        "w_mk": din("w_mk", [D, MEMW]),
        "w_mv": din("w_mv", [D, MEMW]),
        "w_pa": din("w_pa", [SGUW, D]),
        "w_pb": din("w_pb", [POOLW, D]),
        "w_pc": din("w_pc", [MEMW, D]),
        "w_o": din("w_o", [D, D]),
        "w_fg": din("w_fg", [D, DFF]),
        "w_fu": din("w_fu", [D, DFF]),
        "w_fd": din("w_fd", [DFF, D]),
    }
    yp = dout("yp", [SEQ, D])
    ys = dout("ys", [128, D])
    npool_p = dout("npool_p", [PST, POOLW])
    npool_s = dout("npool_s", [2 * PST, POOLW])
    mk_o = dout("mk_o", [NMEM, MEMW])
    mv_o = dout("mv_o", [NMEM, MEMW])
    sv_o = dout("sv_o", [128, SGUW])

    NBLK = 136
    WBP = dscr("wbp", [NBLK, 128, 4096])
    blk_id = {}

    sb = nc.alloc_sbuf_tensor
    XB = [sb("x32a", [128, NS, D], F32), sb("x32b", [128, NS, D], F32)]
    XR = [["xa%d" % s for s in range(NS)], ["xb%d" % s for s in range(NS)]]
    hT = sb("hT", [128, 16, TS], BF16)
    big = sb("big", [128, NFC * TS], BF16)
    FF = sb("FF", [128, 16, TS], BF16)
    NSLOT = 6
    wring = [sb("wr%d" % i, [128, 4096], BF16) for i in range(NSLOT)]
    LX = 432
    xbc = [sb("xbc%d" % i, [128, LX], F32) for i in range(2)]
    ptA = sb("ptA", [128, LX], F32)
    ptB = sb("ptB", [128, LX], F32)
    v32 = sb("v32", [128, 512], F32)
    gtmp = [sb("gt%d" % i, [128, TS], F32) for i in range(2)]
    t2 = sb("t2", [128, TS], F32)
    pexp = sb("pexp", [128, 1024], BF16)
    pn = sb("pn", [128, 1024], BF16)
    wsTs = sb("wsTs", [128, NS, 512], BF16)
    colv = sb("colv_s", [128, NCOLV], F32)
    ident = sb("ident", [128, 128], F32)
    identb = sb("identb", [128, 128], BF16)
    wsT32 = sb("wsT32", [128, 512], F32)
    wsT32s = sb("wsT32s", [128, 512], F32)
    bsT = sb("bsT", [128, 4, 128], F32)
    bsTs = sb("bsTs", [128, 4, 128], F32)
    wpool = sb("wpool_s", [128, 4, 2, 256], BF16)
    gfin = sb("gfin_s", [128, D], F32)
    kT0 = sb("kT", [128, 8, 256], BF16)
    Vb0 = sb("Vb", [128, 2, 1024], BF16)
    prev = sb("prev", [128, 8, 3 * PST], F32)
    invc = sb("invc", [128, 4, 16], F32)
    st = sb("stats", [128, 64], F32)

    merged32 = big[:, 0:16 * TS * 2].bitcast(F32)
    F2 = big[:, 32 * TS:40 * TS]
    FFflat = FF[:].rearrange("p a b -> p (a b)")

    def ff_res(lo, hi):
        return ["FF%d" % j for j in range(lo // TS, (hi - 1) // TS + 1)]

    kbf = FFflat[:, 8 * TS:8 * TS + 2048].rearrange("p (m f) -> p m f", m=2)
    KBF = [ff_res(8 * TS, 8 * TS + 1024), ff_res(8 * TS + 1024, 8 * TS + 2048)]

    BOUNDS = {"w_in": [0, 2048, 4096, 6144, 8192, 10240], "w_pa": [0, 2048], "w_pb": [0, 2048],
              "w_pc": [0, 2048], "w_o": [0, 2048], "w_fg": [0, 1536, 3072, 4608, 5632],
              "w_fu": [0, 1536, 3072, 4608, 5632], "w_fd": [0, 1024, 2048],
              "w_mk": [0, 1024], "w_mv": [0, 1024]}
    conv_state = {"pos": None, "n": 0, "seen": {}}

    NB = 8
    banks = [nc.alloc_psum_tensor("ps%d" % i, [128, 512], F32) for i in range(NB)]
    cnt = {"b": 0, "t": 0, "w": 0, "g": 0, "x": 0}

    def bank():
        i = cnt["b"] % NB
        cnt["b"] += 1
        return banks[i], "PS:b%d" % i

    def tbank():
        bk, br = bank()
        return bk[:].bitcast(BF16), br

    def gt():
        i = cnt["g"] % 2
        cnt["g"] += 1
        return gtmp[i], "gt%d" % i

    wstate = {"t": 0}
    pending_wb = []

    def flush_wb(age):
        while pending_wb and cnt["w"] - pending_wb[0][0] >= age:
            pending_wb.pop(0)[1]()
    WB_TILE = {"w_in": 0, "w_fg": 0, "w_o": 0, "w_fu": 1, "w_fd": 1, "w_pa": 1, "w_pb": 1, "w_pc": 1,
               "w_mk": 99, "w_mv": 99}

    def wload(name, r0, nk, c0, ncol):
        assert nk * ncol <= 4096
        flush_wb(3)
        i = cnt["w"] % NSLOT
        cnt["w"] += 1
        slot = wring[i]
        view = slot[:, 0:nk * ncol].rearrange("p (k c) -> p k c", k=nk)
        bres = "WB:%s:%d:%d" % (name, r0, c0)
        flat = slot[:, 0:nk * ncol]
        if wstate["t"] <= WB_TILE[name]:
            src32 = W32[name][r0:r0 + 128 * nk, c0:c0 + ncol].rearrange("(k p) c -> p k c", p=128)
            P.op("pool", lambda e: e.dma_start(out=view, in_=src32),
                 writes=["W%d" % i], dma_key="ws%d" % i)
            if wstate["t"] == WB_TILE[name]:
                assert bres not in blk_id
                blk_id[bres] = len(blk_id)
                dst = WBP[blk_id[bres], :, 0:nk * ncol]
                pending_wb.append((cnt["w"], lambda dst=dst, flat=flat, i=i, bres=bres: P.op(
                    "sp", lambda e: e.dma_start(out=dst, in_=flat),
                    reads=["W%d" % i], writes=[bres], dma_key="wb%d" % i)))
        else:
            src = WBP[blk_id[bres], :, 0:nk * ncol]
            P.op("sp", lambda e: e.dma_start(out=flat, in_=src),
                 reads=[bres], writes=["W%d" % i], dma_key="w%d" % i)
        return view, "W%d" % i

    P.op("pool", lambda e: e.memset(ident[:], 0.0), writes=["ident"])
    P.op("pool", lambda e: e.affine_select(out=ident[:], in_=ident[:], pattern=[[-1, 128]],
                                           compare_op=ALU.not_equal, fill=1.0, base=0,
                                           channel_multiplier=1),
         reads=["ident"], writes=["ident"])
    P.op("dve", lambda e: e.tensor_copy(out=identb[:], in_=ident[:]),
         reads=["ident"], writes=["identb"])
    P.op("sp", lambda e: e.dma_start(out=colv[:], in_=colv_d), writes=["colv"], dma_key="c0")
    P.op("sp", lambda e: e.dma_start(out=wsT32[:], in_=wsT_d.rearrange("p g i -> p (g i)")),
         writes=["wsT32"], dma_key="c1")
    P.op("sp", lambda e: e.dma_start(out=bsT[:].rearrange("p g i -> p (g i)"),
                                     in_=bsgu_d.partition_broadcast(128)),
         writes=["bsT"], dma_key="c2")
    P.op("sp", lambda e: e.dma_start(out=bsTs[:].rearrange("p g i -> p (g i)"),
                                     in_=bsgus_d.partition_broadcast(128)),
         writes=["bsTs"], dma_key="c3")
    P.op("sp", lambda e: e.dma_start(out=gfin[:], in_=gfin_d.partition_broadcast(128)),
         writes=["gfin"], dma_key="c4")

    def mask_ws(e):
        v = wsT32[:].rearrange("p (g i) -> p g i", g=4)
        return e.memset(v[64:128, :, 0:64], 0.0)
    P.op("dve", mask_ws, reads=["wsT32"], writes=["wsT32"])
    P.op("dve", lambda e: e.memset(wsT32s[:], 0.0), writes=["wsT32s"])

    def ld_wss(e):
        v = wsT32s[:].rearrange("p (g i) -> p g i", g=4)
        a = e.dma_start(out=v[0:64, :, 0:64], in_=wsT_d[0:64, :, 0:64])
        b = e.dma_start(out=v[64:128, :, 64:128], in_=wsT_d[0:64, :, 0:64])
        return [a, b]
    P.op("sp", ld_wss, writes=["wsT32s"], dma_key="c5", ndma=2)
    P.op("pool", lambda e: e.dma_start(
        out=wpool[:].rearrange("p g k c -> p (g k) c"),
        in_=wpool_d.rearrange("g (k p) c -> p (g k) c", p=128)),
        writes=["wpool"], dma_key="c6")

    def mk_inv(e):
        r = []
        for g, w in enumerate(WINS):
            for t in range(16):
                r.append(e.memset(invc[:, g, t:t + 1], 1.0 / min(t + 1, w)))
        return r
    P.op("pool", mk_inv, writes=["invc"])
    P.op("dve", lambda e: e.memset(prev[:], 0.0), writes=["prev"])

    out_ops = []

    def norm_stats(nsub, xres, xview, ssc, presummed=False):
        for s in range(0 if presummed else nsub):
            P.op("act", (lambda s: lambda e: e.activation(
                out=pexp[:, 0:1024], in_=xview(s)[:, 0:1024], func=AF.Square,
                accum_out=st[:, ssc + s:ssc + s + 1]))(s),
                reads=[xres(s)], writes=["pexp", "st%d" % (ssc + s)])
            P.op("act", (lambda s: lambda e: e.activation(
                out=pexp[:, 0:1024], in_=xview(s)[:, 1024:2048], func=AF.Square,
                accum_out=st[:, ssc + 8 + s:ssc + 8 + s + 1]))(s),
                reads=[xres(s)], writes=["pexp", "st%d" % (ssc + 8 + s)])
        r1 = ["st%d" % (ssc + s) for s in range(nsub)]
        rd = r1 + ["st%d" % (ssc + 8 + s) for s in range(nsub)]
        a0, a1 = ssc, ssc + nsub
        if presummed:
            P.op("dve", lambda e: e.tensor_reduce(
                out=st[:, a0:a1], in_=st[:, 0:8 * nsub].rearrange("p (s c) -> p s c", c=8),
                axis=AX.X, op=ALU.add),
                reads=["st%d" % j for j in range(8 * nsub)], writes=r1)
        else:
            P.op("dve", lambda e: e.tensor_tensor(out=st[:, a0:a1], in0=st[:, a0:a1],
                                                  in1=st[:, a0 + 8:a1 + 8], op=ALU.add),
                 reads=rd, writes=r1)
        P.op("dve", lambda e: e.tensor_scalar(out=st[:, a0:a1], in0=st[:, a0:a1],
                                              scalar1=1.0 / D, scalar2=EPS,
                                              op0=ALU.mult, op1=ALU.add), reads=r1, writes=r1)
        P.op("act", lambda e: e.activation(out=st[:, a0:a1], in_=st[:, a0:a1], func=AF.Sqrt),
             reads=r1, writes=r1)
        P.op("dve", lambda e: e.reciprocal(out=st[:, a0:a1], in_=st[:, a0:a1]), reads=r1, writes=r1)
        for s in range(nsub):
            xn = FFflat[:, s * 2048:(s + 1) * 2048]
            P.op("act", (lambda s, xn: lambda e: e.activation(
                out=xn, in_=xview(s), func=AF.Identity, scale=st[:, ssc + s:ssc + s + 1]))(s, xn),
                reads=[xres(s), "st%d" % (ssc + s)], writes=ff_res(s * 2048, (s + 1) * 2048))

    def norm_tr(nsub, gcol):
        for s in range(nsub):
            xn = FFflat[:, s * 2048:(s + 1) * 2048]
            xnres = ff_res(s * 2048, (s + 1) * 2048)
            for half in range(2):
                tb, tbr = tbank()

                def tr(e, xn=xn, half=half, tb=tb):
                    r = []
                    for k in range(8):
                        c = half * 8 + k
                        r.append(e.transpose(out=tb[:, k * 128:(k + 1) * 128],
                                             in_=xn[:, c * 128:(c + 1) * 128],
                                             identity=identb[:]))
                    return r
                P.op("pe", tr, reads=xnres + ["identb"], writes=[tbr])
                P.op("dve", (lambda s, half, tb: lambda e: e.tensor_tensor(
                    out=hT[:, half * 8:half * 8 + 8, s * 128:(s + 1) * 128],
                    in0=tb[:].rearrange("p (k t) -> p k t", k=8),
                    in1=colv[:, gcol + half * 8:gcol + half * 8 + 8].unsqueeze(2).to_broadcast([128, 8, 128]),
                    op=ALU.mult))(s, half, tb),
                    reads=[tbr, "colv"], writes=["hT%d_%d" % (s, half)])

    def hT_res(nsub):
        return ["hT%d_%d" % (s, h) for s in range(nsub) for h in range(2)]

    def mm_fm(wv, wr, ccol, nk, rhs_fn, rhs_res, T):
        bk, br = bank()

        def f(e):
            r = []
            for k in range(nk):
                r.append(e.matmul(bk[:, 0:T], lhsT=wv[:, k, ccol:ccol + 128], rhs=rhs_fn(k),
                                  start=(k == 0), stop=(k == nk - 1)))
            return r
        P.op("pe", f, reads=[wr] + list(rhs_res), writes=[br])
        return bk, br

    def k_transposes(src, src_res, dst, dst_res):
        for mc in range(2):
            tb, tbr = tbank()

            def tr(e, mc=mc, tb=tb):
                r = []
                for c in range(8):
                    r.append(e.transpose(out=tb[:, c * 128:(c + 1) * 128],
                                         in_=src[:, mc, c * 128:(c + 1) * 128],
                                         identity=identb[:]))
                return r
            P.op("pe", tr, reads=list(src_res[mc]) + ["identb"], writes=[tbr])
            P.op("dve", (lambda mc, tb: lambda e: e.tensor_copy(
                out=dst[:, :, mc * 128:(mc + 1) * 128],
                in_=tb[:].rearrange("p (c m) -> p c m", c=8)))(mc, tb),
                reads=[tbr], writes=[dst_res])

    def mem_phase():
        X = XB[0]
        for s in range(2):
            P.op("sp", (lambda s: lambda e: e.dma_start(out=X[:, s, :], in_=mem[s * 128:(s + 1) * 128, :]))(s),
                 writes=[XR[0][s]], dma_key="xl%d" % s)
        norm_stats(2, lambda s: XR[0][s], lambda s: X[:, s, :], 0)
        norm_tr(2, C_GMEM)
        for wi, (wname, dst) in enumerate((("w_mk", mk_o), ("w_mv", mv_o))):
            for cq in range(4):
                wv, wr = wload(wname, 0, 16, cq * 256, 256)
                for s in range(2):
                    bk, br = bank()

                    def f(e, s=s, wv=wv, bk=bk):
                        r = []
                        for k in range(16):
                            r.append(e.matmul(bk[:, 0:256], lhsT=hT[:, k, s * 128:(s + 1) * 128],
                                              rhs=wv[:, k, :], start=(k == 0), stop=(k == 15)))
                        return r
                    P.op("pe", f, reads=[wr] + hT_res(2), writes=[br])
                    g, gr = gt()
                    P.op("act", (lambda g, bk: lambda e: e.activation(out=g[:, 0:256], in_=bk[:, 0:256], func=AF.Identity))(g, bk),
                         reads=[br], writes=[gr])
                    out_ops.append(P.op("sp", (lambda dst, s, cq, g: lambda e: e.dma_start(
                        out=dst[s * 128:(s + 1) * 128, cq * 256:(cq + 1) * 256], in_=g[:, 0:256]))(dst, s, cq, g),
                        reads=[gr], dma_key="mo%d" % (cnt["x"] % 2)))
                    cnt["x"] += 1
                    if wi == 0:
                        P.op("dve", (lambda s, cq, g: lambda e: e.tensor_copy(
                            out=kbf[:, s, cq * 256:(cq + 1) * 256], in_=g[:, 0:256]))(s, cq, g),
                            reads=[gr], writes=KBF[s])
                    else:
                        P.op("dve", (lambda s, cq, g: lambda e: e.tensor_copy(
                            out=Vb0[:, s, cq * 256:(cq + 1) * 256], in_=g[:, 0:256]))(s, cq, g),
                            reads=[gr], writes=["Vb0"])
        k_transposes(kbf, KBF, kT0[:], "kT0")

    ntiles = len(plan)

    def xsrc_ap(sub):
        return xs if sub[0] == "s" else xp[sub[1]:sub[1] + 128, :]

    def ydst_ap(sub):
        return ys if sub[0] == "s" else yp[sub[1]:sub[1] + 128, :]

    def load_x(t, only=None):
        X = XB[t % 2]
        for s, sub in enumerate(plan[t]):
            if only is not None and s != only:
                continue
            P.op("sp", (lambda s, sub: lambda e: e.dma_start(out=X[:, s, :], in_=xsrc_ap(sub)))(s, sub),
                 writes=[XR[t % 2][s]], dma_key="xl%d" % s)

    def prologue_stats(t):
        X = XB[t % 2]
        norm_stats(len(plan[t]), lambda s: XR[t % 2][s], lambda s: X[:, s, :], 0)

    def prologue_tr(t):
        norm_tr(len(plan[t]), C_GMIX)

    def final_steps(t):
        X = XB[t % 2]
        xr = XR[t % 2]
        subs = plan[t]
        nsub = len(subs)
        fst = ["st%d" % (56 + j) for j in range(8)]
        steps = []

        def sq(s):
            for hh in range(2):
                P.op("act", (lambda s, hh: lambda e: e.activation(
                    out=pexp[:, 0:1024], in_=X[:, s, hh * 1024:(hh + 1) * 1024], func=AF.Square,
                    accum_out=st[:, 56 + hh * 4 + s:57 + hh * 4 + s]))(s, hh),
                    reads=[xr[s]], writes=["pexp", "st%d" % (56 + hh * 4 + s)])

        def chain():
            P.op("dve", lambda e: e.tensor_tensor(out=st[:, 56:56 + nsub], in0=st[:, 56:56 + nsub],
                                                  in1=st[:, 60:60 + nsub], op=ALU.add), reads=fst, writes=fst)
            P.op("dve", lambda e: e.tensor_scalar(out=st[:, 56:56 + nsub], in0=st[:, 56:56 + nsub],
                                                  scalar1=1.0 / D, scalar2=EPS, op0=ALU.mult, op1=ALU.add),
                 reads=fst, writes=fst)
            P.op("act", lambda e: e.activation(out=st[:, 56:56 + nsub], in_=st[:, 56:56 + nsub], func=AF.Sqrt),
                 reads=fst, writes=fst)
            P.op("dve", lambda e: e.reciprocal(out=st[:, 56:56 + nsub], in_=st[:, 56:56 + nsub]),
                 reads=fst, writes=fst)

        def yst(s):
            P.op("dve", (lambda s: lambda e: e.scalar_tensor_tensor(
                out=X[:, s, :], in0=X[:, s, :], scalar=st[:, 56 + s:57 + s], in1=gfin[:],
                op0=ALU.mult, op1=ALU.mult))(s),
                reads=[xr[s], "gfin"] + fst, writes=[xr[s]])
            out_ops.append(P.op("pool", (lambda s: lambda e: e.dma_start(
                out=ydst_ap(subs[s]), in_=X[:, s, :]))(s),
                reads=[xr[s]], dma_key="yo%d" % s))
        for s in range(nsub):
            steps.append((lambda s: lambda: sq(s))(s))
        steps.append(chain)
        for s in range(nsub):
            steps.append((lambda s: lambda: yst(s))(s))
        return steps

    def final(t):
        for f_ in final_steps(t):
            f_()

    def pool_out(c0, nrows, dst, key, stage, stage_res):
        for half in range(2):
            bk, br = bank()

            def trp(e, half=half, bk=bk):
                r = []
                for c in range(4):
                    cc = half * 4 + c
                    r.append(e.transpose(out=bk[0:nrows, c * 128:(c + 1) * 128], in_=prev[:, cc, c0:c0 + nrows],
                                         identity=ident[:]))
                return r
            P.op("pe", trp, reads=["prev", "ident"], writes=[br])
            P.op("dve", (lambda half, bk: lambda e: e.tensor_copy(
                out=stage[0:nrows, half * 512:(half + 1) * 512], in_=bk[0:nrows, :]))(half, bk),
                reads=[br], writes=[stage_res])
        out_ops.append(P.op("pool", lambda e: e.dma_start(out=dst, in_=stage[0:nrows, :]),
                            reads=[stage_res], dma_key=key))

    pending = {"final": None}

    def tile(t):
        subs = plan[t]
        nsub = len(subs)
        T = nsub * 128
        X = XB[t % 2]
        xr = XR[t % 2]
        first = (t == 0)
        last = (t == ntiles - 1)
        pl = "dve" if t <= 1 else "pool"
        kinds = [sb_[0] for sb_ in subs]
        has_s = "s" in kinds
        np_ = kinds.count("p")
        assert kinds == ["p"] * np_ + ["s"] * (nsub - np_)
        TP = np_ * 128
        hres = hT_res(nsub)
        hrhs = lambda k: hT[:, k, 0:T]

        if has_s and pending["final"] is not None:
            final(pending["final"])
            pending["final"] = None
        if has_s:
            OB = XB[(t + 1) % 2]
            obr = XR[(t + 1) % 2]
            ob16 = OB[:].rearrange("p a b -> p (a b)").bitcast(BF16)
            kTs = [ob16[:, st_ * 2048:(st_ + 1) * 2048].rearrange("p (c m) -> p c m", c=8) for st_ in range(2)]
            Vbs = [ob16[:, 4096 + st_ * 2048:4096 + (st_ + 1) * 2048].rearrange("p (c f) -> p c f", c=2) for st_ in range(2)]
            kbs = ob16[:, 8192:10240].rearrange("p (m f) -> p m f", m=2)
            spl = OB[0:32, 2, 1024:2048]
            for st_ in range(2):
                P.op("pool", (lambda st_: lambda e: e.dma_start(
                    out=kbs, in_=ck[st_].rearrange("(c p) f -> p c f", p=128)))(st_),
                    writes=[obr[2]], dma_key="kl")
                k_transposes(kbs, [[obr[2]], [obr[2]]], kTs[st_], obr[0])
                P.op("pool", (lambda st_: lambda e: e.dma_start(
                    out=Vbs[st_], in_=cv[st_].rearrange("(c p) f -> p c f", p=128)))(st_),
                    writes=[obr[1]], dma_key="vl")
            P.op("sp", lambda e: e.dma_start(out=spl[0:2 * PST, :], in_=spool), writes=[obr[2]], dma_key="c7")
            for half in range(2):
                bk, br = bank()

                def trp(e, half=half, bk=bk):
                    r = []
                    for c in range(4):
                        cc = half * 4 + c
                        r.append(e.transpose(out=bk[:, c * 32:c * 32 + 2 * PST],
                                             in_=spl[0:2 * PST, cc * 128:(cc + 1) * 128],
                                             identity=ident[0:2 * PST, 0:2 * PST]))
                    return r
                P.op("pe", trp, reads=[obr[2], "ident"], writes=[br])
                P.op("dve", (lambda half, bk: lambda e: e.tensor_copy(
                    out=prev[:, half * 4:half * 4 + 4, PST:3 * PST],
                    in_=bk[:, 0:128].rearrange("p (c t) -> p c t", c=4)[:, :, 0:2 * PST]))(half, bk),
                    reads=[br], writes=["prev"])
            kv_view = [(kT0[:], "kT0", Vb0[:], "Vb0"), (kTs[0], obr[0], Vbs[0], obr[1]), (kTs[1], obr[0], Vbs[1], obr[1])]
        else:
            kv_view = [(kT0[:], "kT0", Vb0[:], "Vb0")]

        segs = []
        b0 = 0
        if TP:
            segs.append((b0, TP, 0, 0))
            b0 += PST + TP
        if has_s:
            segs.append((b0, 64, TP, PST))
            b0 += PST + 64
            segs.append((b0, 64, TP + 64, 2 * PST))
            b0 += PST + 64
        L = b0
        assert L <= LX
        agrp = [(s * 128, 128, 0) for s in range(np_)]
        if has_s:
            agrp += [(TP, 64, 1), (TP + 64, 64, 2)]
        kvr = []
        if TP:
            kvr.append((0, TP, 0))
        if has_s:
            kvr += [(TP, 64, 1), (TP + 64, 64, 2)]

        vbase = 8 * TS
        vview = FFflat[:, vbase:vbase + nsub * 1024]
        for q in range(4):
            wv, wr = wload("w_in", 0, 16, OFF_V + q * 256, 256)
            for s in range(nsub):
                bk, br = bank()
                vb = v32[:, (cnt["x"] % 2) * 256:(cnt["x"] % 2) * 256 + 256]
                vbr = "v32_%d" % (cnt["x"] % 2)
                cnt["x"] += 1

                def f(e, s=s, wv=wv, bk=bk):
                    r = []
                    for k in range(16):
                        r.append(e.matmul(bk[:, 0:256], lhsT=hT[:, k, s * 128:(s + 1) * 128],
                                          rhs=wv[:, k, :], start=(k == 0), stop=(k == 15)))
                    return r
                P.op("pe", f, reads=[wr] + hres, writes=[br])
                P.op("act", (lambda bk, vb: lambda e: e.activation(out=vb, in_=bk[:, 0:256], func=AF.Gelu_apprx_tanh))(bk, vb),
                     reads=[br], writes=[vbr])
                if kinds[s] == "s":
                    out_ops.append(P.op("pool", (lambda q, vb: lambda e: e.dma_start(
                        out=sv_o[:, q * 256:(q + 1) * 256], in_=vb))(q, vb),
                        reads=[vbr], dma_key="svo"))
                P.op("act", (lambda s, q, vb: lambda e: e.activation(
                    out=pexp[:, 0:256], in_=vb, func=AF.Square,
                    accum_out=st[:, 16 + q * 4 + s:16 + q * 4 + s + 1]))(s, q, vb),
                    reads=[vbr], writes=["pexp", "st%d" % (16 + q * 4 + s)])
                lo = vbase + s * 1024 + q * 256
                P.op("dve", (lambda s, q, vb: lambda e: e.tensor_copy(
                    out=vview[:, s * 1024 + q * 256:s * 1024 + (q + 1) * 256], in_=vb))(s, q, vb),
                    reads=[vbr], writes=ff_res(lo, lo + 256))
        def v_stats_chain():
            vst = ["st%d" % (16 + j) for j in range(16)]
            for j in (1, 2, 3):
                P.op("dve", (lambda j: lambda e: e.tensor_tensor(
                    out=st[:, 16:16 + nsub], in0=st[:, 16:16 + nsub],
                    in1=st[:, 16 + 4 * j:16 + 4 * j + nsub], op=ALU.add))(j),
                    reads=vst, writes=vst)
            P.op("dve", lambda e: e.tensor_scalar(out=st[:, 16:16 + nsub], in0=st[:, 16:16 + nsub],
                                                  scalar1=1.0 / SGUW, scalar2=EPS, op0=ALU.mult, op1=ALU.add),
                 reads=vst, writes=vst)
            P.op("act", lambda e: e.activation(out=st[:, 16:16 + nsub], in_=st[:, 16:16 + nsub], func=AF.Sqrt),
                 reads=vst, writes=vst)
            P.op("dve", lambda e: e.reciprocal(out=st[:, 16:16 + nsub], in_=st[:, 16:16 + nsub]),
                 reads=vst, writes=vst)
            for s in range(nsub):
                wsrc = wsT32s if kinds[s] == "s" else wsT32
                P.op("dve", (lambda s, wsrc: lambda e: e.tensor_scalar(
                    out=wsTs[:, s, :], in0=wsrc[:], scalar1=st[:, 16 + s:17 + s], scalar2=None,
                    op0=ALU.mult))(s, wsrc),
                    reads=vst + ["wsT32", "wsT32s"], writes=["wsTs%d" % s])
        for q in range(4):
            if q == 2:
                v_stats_chain()
            wv, wr = wload("w_in", 0, 16, OFF_U + q * 256, 256)
            for c4 in range(2):
                fc = q * 2 + c4
                bk, br = mm_fm(wv, wr, c4 * 128, 16, hrhs, hres, T)
                P.op("act", (lambda fc, bk: lambda e: e.activation(
                    out=FF[:, fc, 0:T], in_=bk[:, 0:T], func=AF.Gelu_apprx_tanh))(fc, bk),
                    reads=[br], writes=["FF%d" % fc])
        fsteps = []
        if pending["final"] is not None:
            fsteps = final_steps(pending["final"])
            pending["final"] = None
        vres_all = ff_res(vbase, vbase + nsub * 1024)
        for fc in range(8):
            g = fc // 2
            bk, br = bank()

            def f(e, fc=fc, g=g, bk=bk):
                r = []
                for s in range(nsub):
                    r.append(e.matmul(bk[:, s * 128:(s + 1) * 128],
                                      lhsT=vview[:, s * 1024 + fc * 128:s * 1024 + (fc + 1) * 128],
                                      rhs=wsTs[:, s, g * 128:(g + 1) * 128], start=True, stop=True))
                return r
            P.op("pe", f, reads=vres_all + ["wsTs%d" % s for s in range(nsub)], writes=[br])

            def comb(e, fc=fc, g=g, bk=bk):
                r = []
                if np_:
                    r.append(e.scalar_tensor_tensor(
                        out=t2[:, 0:TP].rearrange("p (s i) -> p s i", s=np_),
                        in0=bk[:, 0:TP].rearrange("p (s i) -> p s i", s=np_),
                        scalar=colv[:, C_GV + fc:C_GV + fc + 1],
                        in1=bsT[:, g:g + 1, :].to_broadcast([128, np_, 128]),
                        op0=ALU.mult, op1=ALU.add))
                if has_s:
                    r.append(e.scalar_tensor_tensor(
                        out=t2[:, TP:T], in0=bk[:, TP:T],
                        scalar=colv[:, C_GV + fc:C_GV + fc + 1],
                        in1=bsTs[:, g, :], op0=ALU.mult, op1=ALU.add))
                return r
            P.op("dve", comb, reads=[br, "colv", "bsT", "bsTs"], writes=["t2"])
            P.op("dve", (lambda fc: lambda e: e.tensor_tensor(
                out=FF[:, fc, 0:T], in0=FF[:, fc, 0:T], in1=t2[:, 0:T], op=ALU.mult))(fc),
                reads=["t2", "FF%d" % fc], writes=["FF%d" % fc])

        bblk = {}

        def b_chunk(c):
            q, c4 = c // 2, c % 2
            if c4 == 0:
                bblk["w"] = wload("w_in", 0, 16, OFF_B + q * 256, 256)
            wv, wr = bblk["w"]
            g = c // 2
            w = WINS[g]
            bk, br = mm_fm(wv, wr, c4 * 128, 16, hrhs, hres, T)
            xb = xbc[c % 2]
            xbr = "xbc%d" % (c % 2)

            def cp_act(e, c=c, bk=bk, xb=xb):
                r = []
                for (sb0, nt, pc0, pv0) in segs:
                    r.append(e.activation(out=xb[:, sb0:sb0 + PST], in_=prev[:, c, pv0:pv0 + PST],
                                          func=AF.Identity))
                    r.append(e.activation(out=xb[:, sb0 + PST:sb0 + PST + nt],
                                          in_=bk[:, pc0:pc0 + nt], func=AF.Identity))
                return r
            P.op("act", cp_act, reads=[br, "prev"], writes=[xbr])

            def sv(e, c=c, xb=xb):
                r = []
                for (sb0, nt, pc0, pv0) in segs:
                    r.append(e.tensor_copy(out=prev[:, c, pv0:pv0 + PST],
                                           in_=xb[:, sb0 + nt:sb0 + nt + PST]))
                return r
            P.op("dve", sv, reads=[xbr], writes=["prev"])
            P.op(pl, (lambda xb: lambda e: e.tensor_tensor(
                out=ptA[:, 1:L], in0=xb[:, 1:L], in1=xb[:, 0:L - 1], op=ALU.add))(xb),
                reads=[xbr], writes=["ptA"])
            if w >= 4:
                P.op(pl, lambda e: e.tensor_tensor(
                    out=ptB[:, 3:L], in0=ptA[:, 3:L], in1=ptA[:, 1:L - 2], op=ALU.add),
                    reads=["ptA"], writes=["ptB"])
            if w >= 8:
                P.op(pl, lambda e: e.tensor_tensor(
                    out=ptA[:, 7:L], in0=ptB[:, 7:L], in1=ptB[:, 3:L - 4], op=ALU.add),
                    reads=["ptB"], writes=["ptA"])
            if w >= 16:
                P.op(pl, lambda e: e.tensor_tensor(
                    out=ptB[:, 15:L], in0=ptA[:, 15:L], in1=ptA[:, 7:L - 8], op=ALU.add),
                    reads=["ptA"], writes=["ptB"])
            sres = ptB if w in (4, 16) else ptA

            def fin(e, c=c, g=g, w=w, xb=xb, sres=sres):
                r = []
                for (sb0, nt, pc0, pv0) in segs:
                    r.append(e.scalar_tensor_tensor(
                        out=F2[:, c * TS + pc0:c * TS + pc0 + nt], in0=sres[:, sb0 + PST:sb0 + PST + nt],
                        scalar=1.0 / w, in1=xb[:, sb0 + PST:sb0 + PST + nt],
                        op0=ALU.mult, op1=ALU.subtract))
                return r
            P.op("dve", fin, reads=["ptA", "ptB", xbr], writes=["big%d" % (32 + c)])
            if first:
                P.op("dve", (lambda g, sres: lambda e: e.tensor_tensor(
                    out=t2[:, 0:PST], in0=sres[:, PST:2 * PST], in1=invc[:, g, 0:PST], op=ALU.mult))(g, sres),
                    reads=["ptA", "ptB", "invc"], writes=["t2"])
                P.op("dve", (lambda c, xb: lambda e: e.tensor_tensor(
                    out=F2[:, c * TS:c * TS + PST], in0=t2[:, 0:PST], in1=xb[:, PST:2 * PST], op=ALU.subtract))(c, xb),
                    reads=["t2", xbr], writes=["big%d" % (32 + c)])
        def proj_gate(bi, wname, src_fn, src_res, extra=None):
            for q in range(4):
                pv, pr = wload(wname, 0, 8, q * 512, 512)
                gblk = [None, None]
                for c4 in range(4):
                    fc = q * 4 + c4
                    if c4 % 2 == 0:
                        h2 = c4 // 2
                        gblk[h2] = wload("w_in", 0, 16, OFF_G + bi * D + q * 512 + h2 * 256, 256)
                    if extra is not None:
                        extra(fc)
                    gv_, gr_ = gblk[c4 // 2]
                    yb_, ybr = mm_fm(pv, pr, c4 * 128, 8, src_fn, src_res, T)
                    gb_, gbr = mm_fm(gv_, gr_, (c4 % 2) * 128, 16, hrhs, hres, T)
                    g, gr = gt()
                    P.op("act", (lambda fc, gb_, g: lambda e: e.activation(
                        out=g[:, 0:T], in_=gb_[:, 0:T], func=AF.Sigmoid,
                        bias=colv[:, C_BG + bi * 16 + fc:C_BG + bi * 16 + fc + 1]))(fc, gb_, g),
                        reads=[gbr, "colv"], writes=[gr])
                    mres = ["big%d" % (2 * fc), "big%d" % (2 * fc + 1)]
                    m32 = merged32[:, fc * TS:fc * TS + T]
                    if bi == 0:
                        P.op("dve", (lambda g, yb_, m32: lambda e: e.tensor_tensor(
                            out=m32, in0=g[:, 0:T], in1=yb_[:, 0:T], op=ALU.mult))(g, yb_, m32),
                            reads=[gr, ybr], writes=mres)
                    elif bi == 1:
                        P.op("dve", (lambda g, yb_: lambda e: e.tensor_tensor(
                            out=g[:, 0:T], in0=g[:, 0:T], in1=yb_[:, 0:T], op=ALU.mult))(g, yb_),
                            reads=[gr, ybr], writes=[gr])
                        P.op(pl, (lambda g, m32: lambda e: e.tensor_tensor(
                            out=m32, in0=m32, in1=g[:, 0:T], op=ALU.add))(g, m32),
                            reads=[gr] + mres, writes=mres)
                    else:
                        P.op("dve", (lambda g, yb_: lambda e: e.tensor_tensor(
                            out=g[:, 0:T], in0=g[:, 0:T], in1=yb_[:, 0:T], op=ALU.mult))(g, yb_),
                            reads=[gr, ybr], writes=[gr])
                        P.op(pl, (lambda fc, g, m32: lambda e: e.tensor_tensor(
                            out=FF[:, fc, 0:T], in0=m32, in1=g[:, 0:T], op=ALU.add))(fc, g, m32),
                            reads=[gr] + mres, writes=["FF%d" % fc])

        nfs = len(fsteps)
        fsched = {}
        if nfs:
            nsq = (nfs - 1) // 2
            for i_ in range(nsq + 1):
                fsched[i_] = i_
            for i_ in range(nsq):
                fsched[5 + 4 * i_] = nsq + 1 + i_
        fdone = set()

        def extra_a(fc):
            if fc % 2 == 0:
                b_chunk(fc // 2)
            if fc in fsched and fsched[fc] < nfs:
                fsteps[fsched[fc]]()
                fdone.add(fsched[fc])
        proj_gate(0, "w_pa", lambda k: FF[:, k, 0:T], ["FF%d" % k for k in range(8)], extra=extra_a)
        for i_ in range(nfs):
            if i_ not in fdone:
                fsteps[i_]()

        for g in range(4):
            for dc in range(2):
                bk, br = bank()

                def f(e, g=g, dc=dc, bk=bk):
                    r = []
                    for cc in range(2):
                        r.append(e.matmul(bk[:, 0:T], lhsT=wpool[:, g, cc, dc * 128:(dc + 1) * 128],
                                          rhs=F2[:, (2 * g + cc) * TS:(2 * g + cc) * TS + T], start=(cc == 0), stop=(cc == 1)))
                    return r
                P.op("pe", f, reads=["wpool", "big%d" % (32 + 2 * g), "big%d" % (33 + 2 * g)], writes=[br])
                c = 2 * g + dc
                P.op("dve", (lambda c, bk: lambda e: e.tensor_scalar(
                    out=FF[:, 8 + c, 0:T], in0=bk[:, 0:T],
                    scalar1=colv[:, C_PSC + c:C_PSC + c + 1], scalar2=None, op0=ALU.mult))(c, bk),
                    reads=[br, "colv"], writes=["FF%d" % (8 + c)])
        proj_gate(1, "w_pb", lambda k: FF[:, 8 + k, 0:T], ["FF%d" % (8 + k) for k in range(8)])

        for q in range(4):
            wv, wr = wload("w_in", 0, 16, OFF_Q + q * 256, 256)
            for c4 in range(2):
                c = q * 2 + c4
                bk, br = mm_fm(wv, wr, c4 * 128, 16, hrhs, hres, T)
                P.op("act", (lambda c, bk: lambda e: e.activation(
                    out=FF[:, c, 0:T], in_=bk[:, 0:T], func=AF.Identity, scale=0.0625))(c, bk),
                    reads=[br], writes=["FF%d" % c])
        for bi0 in range(0, len(agrp), 3):
            batch = agrp[bi0:bi0 + 3]
            scb = []
            for (c0, tn, kvi) in batch:
                kTv, kTr, _, _ = kv_view[kvi]
                b0_, b0r = bank()
                b1_, b1r = bank()

                def sc(e, c0=c0, tn=tn, kTv=kTv, b0_=b0_, b1_=b1_):
                    r = []
                    for h in range(4):
                        bk = b0_ if h < 2 else b1_
                        hh = h % 2
                        for dc in range(2):
                            r.append(e.matmul(bk[0:tn, hh * 256:(hh + 1) * 256],
                                              lhsT=FF[:, h * 2 + dc, c0:c0 + tn],
                                              rhs=kTv[:, h * 2 + dc, :], start=(dc == 0), stop=(dc == 1)))
                    return r
                P.op("pe", sc, reads=["FF%d" % k for k in range(8)] + [kTr], writes=[b0r, b1r])
                scb.append((b0_, b0r, b1_, b1r))
            v32b = v32[:].bitcast(BF16)
            PBUF = [(pexp, ["pexp"]), (pn, ["pn"]), (v32b, ["v32_0", "v32_1"])]
            MXC = [36, 32, 3]
            SMC = [40, 47, 11]
            for j, (c0, tn, kvi) in enumerate(batch):
                b0_, b0r, b1_, b1r = scb[j]
                mxc = MXC[j]
                mxr = "mx%d" % j
                P.op("dve", (lambda tn, b0_, mxc: lambda e: e.tensor_reduce(
                    out=st[0:tn, mxc:mxc + 2], in_=b0_[0:tn, :].rearrange("p (h m) -> p h m", h=2), axis=AX.X, op=ALU.max))(tn, b0_, mxc),
                    reads=[b0r], writes=[mxr + "a"])
                P.op("dve", (lambda tn, b1_, mxc: lambda e: e.tensor_reduce(
                    out=st[0:tn, mxc + 2:mxc + 4], in_=b1_[0:tn, :].rearrange("p (h m) -> p h m", h=2), axis=AX.X, op=ALU.max))(tn, b1_, mxc),
                    reads=[b1r], writes=[mxr + "b"])
                P.op("dve", (lambda tn, mxc: lambda e: e.tensor_scalar(out=st[0:tn, mxc:mxc + 4], in0=st[0:tn, mxc:mxc + 4], scalar1=-1.0,
                                                                    scalar2=None, op0=ALU.mult))(tn, mxc),
                     reads=[mxr + "a", mxr + "b"], writes=[mxr + "a", mxr + "b"])
            for j, (c0, tn, kvi) in enumerate(batch):
                b0_, b0r, b1_, b1r = scb[j]
                pb, pbr = PBUF[j]
                mxc, smc = MXC[j], SMC[j]
                mxr, smr = "mx%d" % j, "sm%d" % j

                def ex(e, tn=tn, b0_=b0_, b1_=b1_, pb=pb, mxc=mxc, smc=smc):
                    r = []
                    for h in range(4):
                        bk = b0_ if h < 2 else b1_
                        hh = h % 2
                        r.append(e.activation(out=pb[0:tn, h * 256:(h + 1) * 256],
                                              in_=bk[0:tn, hh * 256:(hh + 1) * 256], func=AF.Exp,
                                              bias=st[0:tn, mxc + h:mxc + h + 1], accum_out=st[0:tn, smc + h:smc + h + 1]))
                    return r
                P.op("act", ex, reads=[b0r, b1r, mxr + "a", mxr + "b"], writes=pbr + [smr])
            for j, (c0, tn, kvi) in enumerate(batch):
                pb, pbr = PBUF[j]
                smc = SMC[j]
                smr = "sm%d" % j
                P.op("dve", (lambda tn, smc: lambda e: e.reciprocal(out=st[0:tn, smc:smc + 4], in_=st[0:tn, smc:smc + 4]))(tn, smc),
                     reads=[smr], writes=[smr])
                P.op("dve", (lambda tn, pb, smc: lambda e: e.tensor_tensor(
                    out=pb[0:tn, :].rearrange("p (h m) -> p h m", h=4),
                    in0=pb[0:tn, :].rearrange("p (h m) -> p h m", h=4),
                    in1=st[0:tn, smc:smc + 4].unsqueeze(2).to_broadcast([tn, 4, 256]), op=ALU.mult))(tn, pb, smc),
                    reads=pbr + [smr], writes=pbr)
            tbs = []
            for j, (c0, tn, kvi) in enumerate(batch):
                pb, pbr = PBUF[j]
                tb, tbr = tbank()

                def trp(e, tn=tn, tb=tb, pb=pb):
                    r = []
                    for jj in range(8):
                        r.append(e.transpose(out=tb[:, jj * 128:jj * 128 + tn], in_=pb[0:tn, jj * 128:(jj + 1) * 128],
                                             identity=identb[0:tn, 0:tn]))
                    return r
                P.op("pe", trp, reads=pbr + ["identb"], writes=[tbr])
                tbs.append((tb, tbr))

                def emit_copy(jx):
                    c0x, tnx, _ = batch[jx]
                    tbx, tbrx = tbs[jx]
                    P.op("dve", (lambda c0x, tnx, tbx: lambda e: e.tensor_copy(
                        out=FF[:, 8:16, c0x:c0x + tnx],
                        in_=tbx[:].rearrange("p (j t) -> p j t", j=8)[:, :, 0:tnx]))(c0x, tnx, tbx),
                        reads=[tbrx], writes=["FF%d" % (8 + jj) for jj in range(8)])
                if j >= 1:
                    emit_copy(j - 1)
            emit_copy(len(batch) - 1)
        for h in range(4):
            for dc in range(2):
                bk, br = bank()

                def f(e, h=h, dc=dc, bk=bk):
                    r = []
                    for (c0, ncl, kvi) in kvr:
                        Vv = kv_view[kvi][2]
                        for mc in range(2):
                            r.append(e.matmul(bk[:, c0:c0 + ncl],
                                              lhsT=Vv[:, mc, h * 256 + dc * 128:h * 256 + (dc + 1) * 128],
                                              rhs=FF[:, 8 + h * 2 + mc, c0:c0 + ncl], start=(mc == 0), stop=(mc == 1)))
                    return r
                P.op("pe", f, reads=list({kv_view[kvi][3] for (_, _, kvi) in kvr}) + ["FF%d" % (8 + h * 2), "FF%d" % (9 + h * 2)],
                     writes=[br])
                c = h * 2 + dc
                P.op("act", (lambda c, bk: lambda e: e.activation(
                    out=F2[:, c * TS:c * TS + T], in_=bk[:, 0:T], func=AF.Identity))(c, bk),
                    reads=[br], writes=["big%d" % (32 + c)])
        proj_gate(2, "w_pc", lambda k: F2[:, k * TS:k * TS + T], ["big%d" % (32 + k) for k in range(8)])

        for cq in range(8):
            wv, wr = wload("w_o", 0, 16, cq * 256, 256)
            for s in range(nsub):
                bk, br = bank()

                def f(e, s=s, wv=wv, bk=bk):
                    r = []
                    for k in range(16):
                        r.append(e.matmul(bk[:, 0:256], lhsT=FF[:, k, s * 128:(s + 1) * 128], rhs=wv[:, k, :],
                                          start=(k == 0), stop=(k == 15)))
                    return r
                P.op("pe", f, reads=[wr] + ["FF%d" % k for k in range(16)], writes=[br])
                P.op("dve", (lambda s, cq, bk: lambda e: e.tensor_tensor(
                    out=X[:, s, cq * 256:(cq + 1) * 256], in0=X[:, s, cq * 256:(cq + 1) * 256],
                    in1=bk[:, 0:256], op=ALU.add))(s, cq, bk),
                    reads=[br, xr[s]], writes=[xr[s]])
                P.op("act", (lambda s, cq: lambda e: e.activation(
                    out=pexp[:, 0:256], in_=X[:, s, cq * 256:(cq + 1) * 256], func=AF.Square,
                    accum_out=st[:, s * 8 + cq:s * 8 + cq + 1]))(s, cq),
                    reads=[xr[s]], writes=["pexp", "st%d" % (s * 8 + cq)])

        norm_stats(nsub, lambda s: xr[s], lambda s: X[:, s, :], 44, presummed=True)
        norm_tr(nsub, C_GFFN)
        for q in range(NFC // 2):
            if not last and q % 7 == 1 and q // 7 < len(plan[t + 1]):
                load_x(t + 1, only=q // 7)
            gv_, gr_ = wload("w_fg", 0, 16, q * 256, 256)
            uv_, ur_ = wload("w_fu", 0, 16, q * 256, 256)
            for c4 in range(2):
                fc = q * 2 + c4
                gb_, gbr = mm_fm(gv_, gr_, c4 * 128, 16, hrhs, hres, T)
                ub_, ubr = mm_fm(uv_, ur_, c4 * 128, 16, hrhs, hres, T)
                g, gr = gt()
                P.op("act", (lambda gb_, g: lambda e: e.activation(out=g[:, 0:T], in_=gb_[:, 0:T], func=AF.Silu))(gb_, g),
                     reads=[gbr], writes=[gr])
                P.op("dve", (lambda fc, g, ub_: lambda e: e.tensor_tensor(
                    out=big[:, fc * TS:fc * TS + T], in0=g[:, 0:T], in1=ub_[:, 0:T], op=ALU.mult))(fc, g, ub_),
                    reads=[gr, ubr], writes=["big%d" % fc])
        if not last:
            prologue_stats(t + 1)
        for cq in range(8):
            if cq == 4 and not last:
                prologue_tr(t + 1)
            bks = [bank() for _ in range(nsub)]
            for part in range(4):
                wv, wr = wload("w_fd", part * 11 * 128, 11, cq * 256, 256)
                for s in range(nsub):
                    bk, br = bks[s]

                    def f(e, s=s, part=part, wv=wv, bk=bk):
                        r = []
                        for k in range(11):
                            kk = part * 11 + k
                            r.append(e.matmul(bk[:, 0:256], lhsT=big[:, kk * TS + s * 128:kk * TS + (s + 1) * 128],
                                              rhs=wv[:, k, :], start=(kk == 0), stop=(kk == NFC - 1)))
                        return r
                    P.op("pe", f, reads=[wr] + ["big%d" % (part * 11 + k) for k in range(11)], writes=[br])
            for s in range(nsub):
                bk, br = bks[s]
                P.op("dve", (lambda s, cq, bk: lambda e: e.tensor_tensor(
                    out=X[:, s, cq * 256:(cq + 1) * 256], in0=X[:, s, cq * 256:(cq + 1) * 256],
                    in1=bk[:, 0:256], op=ALU.add))(s, cq, bk),
                    reads=[br, xr[s]], writes=[xr[s]])
        pending["final"] = t

    mem_phase()
    load_x(0)
    prologue_stats(0)
    prologue_tr(0)
    for t in range(ntiles):
        wstate["t"] = t
        tile(t)
    final(pending["final"])
    lastb = XB[(ntiles - 1) % 2]
    stg = XB[ntiles % 2]
    stg_r = XR[ntiles % 2]
    pool_out(0, PST, npool_p, "npo_p", stg[0:32, 0, 0:1024], stg_r[0])
    if any(sub[0] == "s" for tl in plan for sub in tl):
        pool_out(PST, 2 * PST, npool_s, "npo_s", stg[0:32, 0, 1024:2048], stg_r[0])
    flush_wb(0)
    P.finish_on("pool", out_ops)
    P.emit()
    return nc


_NC_CACHE = {}


def _get_nc():
    if "nc" not in _NC_CACHE:
        _NC_CACHE["nc"] = build_program()
    return _NC_CACHE["nc"]


def make_in_maps(inp):
    f = lambda a: np.ascontiguousarray(np.asarray(a, dtype=np.float32))
    colv = np.concatenate([f(inp["g_mix"])[0], f(inp["g_ffn"])[0], f(inp["b_gate"])[0],
                           f(inp["pool_scale"])[0], f(inp["g_sgu_v"])[0], f(inp["g_mem"])[0]])
    colv = np.ascontiguousarray(colv.reshape(NCOLV, 128).T)
    w_sgu = f(inp["w_sgu"])[0]
    wsT = np.ascontiguousarray(np.transpose(w_sgu, (2, 0, 1)))
    b_sgu = f(inp["b_sgu"])[0]
    bsgus = np.ascontiguousarray(np.concatenate([b_sgu[:, :64], b_sgu[:, :64]], axis=1)).reshape(-1)
    shared = {
        "colv": colv, "wsT": wsT, "bsgu": np.ascontiguousarray(b_sgu.reshape(-1)), "bsgus": bsgus,
        "wpool": f(inp["w_pool"])[0], "gfin": f(inp["g_final"]),
        "w_in": f(inp["w_in"])[0], "w_mk": f(inp["w_mk"])[0], "w_mv": f(inp["w_mv"])[0],
        "w_pa": f(inp["w_pa"])[0], "w_pb": f(inp["w_pb"])[0], "w_pc": f(inp["w_pc"])[0],
        "w_o": f(inp["w_o"])[0], "w_fg": f(inp["w_ff_gate"])[0], "w_fu": f(inp["w_ff_up"])[0],
        "w_fd": f(inp["w_ff_down"])[0],
    }
    xp, xs, mem = f(inp["x_prompt"]), f(inp["x_sample"]), f(inp["mem_prompt"])
    sp, ck, cv = f(inp["state_pool"])[0], f(inp["cache_mem_k"])[0], f(inp["cache_mem_v"])[0]
    maps = []
    for c in range(NCORE):
        m = dict(shared)
        m["xp"] = xp[c]
        m["xs"] = np.ascontiguousarray(xs[2 * c:2 * c + 2].reshape(128, D))
        m["mem"] = mem[c]
        m["spool"] = np.ascontiguousarray(sp[2 * c:2 * c + 2].reshape(2 * PST, POOLW))
        m["ck"] = np.ascontiguousarray(ck[2 * c:2 * c + 2].reshape(2, NMEM, MEMW))
        m["cv"] = np.ascontiguousarray(cv[2 * c:2 * c + 2].reshape(2, NMEM, MEMW))
        maps.append(m)
    return maps


def kernel(**inp):
    nc = _get_nc()
    maps = make_in_maps(inp)
    res = run_bass_kernel_spmd(nc, maps, core_ids=list(range(NCORE)))
    R = res.results
    y_prompt = np.stack([R[c]["yp"] for c in range(NCORE)]).astype(np.float32)
    y_sample = np.concatenate([R[c]["ys"].reshape(2, 64, D) for c in range(NCORE)]).astype(np.float32)
    npp = np.stack([R[c]["npool_p"] for c in range(NCORE)])[None].astype(np.float32)
    nps = np.concatenate([R[c]["npool_s"].reshape(2, PST, POOLW) for c in range(NCORE)])[None].astype(np.float32)
    mk = np.stack([R[c]["mk_o"].reshape(NMEM, 4, 256) for c in range(NCORE)])[None].astype(np.float32)
    mv = np.stack([R[c]["mv_o"].reshape(NMEM, 4, 256) for c in range(NCORE)])[None].astype(np.float32)
    sv = np.concatenate([R[c]["sv_o"].reshape(2, 64, SGUW) for c in range(NCORE)])[None].astype(np.float32)
    return (y_prompt, y_sample, npp, nps, mk, mv, sv)
```
